# Optimizing a Trainium2 kernel written in Bass

```python
import math
import jax, jax.numpy as jnp
from jax import lax
import numpy as np

D_MODEL = 1024
BATCH = 8
SEQ = 2048
DEPTH = 4
DEC_BATCH = 32
DEC_SEQ = 1
PAST_LEN = 8192
PAGE_SIZE = 128

SSM_EXPAND = 2
D_INNER = SSM_EXPAND * D_MODEL
SSM_HEAD_DIM = 64
SSM_HEADS = D_INNER // SSM_HEAD_DIM
SSM_GROUPS = 4
SSM_HPG = SSM_HEADS // SSM_GROUPS
D_STATE = 128
SSM_CONV = 4
CONV_DIM = D_INNER + 2 * SSM_GROUPS * D_STATE
SSM_IN_DIM = D_INNER + CONV_DIM + SSM_HEADS
SSM_CHUNK = 128

ATT_GROUPS = ((128, 1), (512, 4), (2048, 16))
ATT_HPG = 4
ATT_HEAD_DIM = 64
ATT_HEADS = ATT_HPG * len(ATT_GROUPS)
ATT_WIDTH = ATT_HEADS * ATT_HEAD_DIM
ROT_DIM = ATT_HEAD_DIM // 4
ROPE_THETA = 500000.0

N_MEM = 256
MEM_HEADS = 4
MEM_HEAD_DIM = D_MODEL // MEM_HEADS

D_FF = 2816
FFN_CONV = 3

EPS = 1e-6

kernel_name = "hybrid_ssd_dilated_swa_decoder_step"


def rms_norm(x, g):
    xf = x.astype(jnp.float32)
    y = xf * lax.rsqrt(jnp.mean(xf * xf, axis=-1, keepdims=True) + EPS)
    return (y * g.astype(jnp.float32)).astype(x.dtype)


def rope_partial(x, pos):
    half = ROT_DIM // 2
    inv = ROPE_THETA ** (-jnp.arange(half, dtype=jnp.float32) / half)
    ang = pos.astype(jnp.float32)[:, None] * inv[None, :]
    cos = jnp.cos(ang)[None, :, None, :]
    sin = jnp.sin(ang)[None, :, None, :]
    xr = x[..., :ROT_DIM].astype(jnp.float32)
    x1, x2 = xr[..., :half], xr[..., half:]
    rot = jnp.concatenate([x1 * cos - x2 * sin, x2 * cos + x1 * sin], axis=-1)
    return jnp.concatenate([rot.astype(x.dtype), x[..., ROT_DIM:]], axis=-1)


def causal_dwconv(xp, w, b):
    k_w = w.shape[0]
    t_len = xp.shape[1] - k_w + 1
    out = xp[:, 0:t_len] * w[0] + b
    for k in range(1, k_w):
        out = out + xp[:, k:k + t_len] * w[k]
    return out


def ssd_scan(xdt, a, bm, cm, h0, chunk):
    n, t_len, g, j, p = xdt.shape
    nc = t_len // chunk
    xdt = xdt.reshape(n, nc, chunk, g, j, p)
    a = a.reshape(n, nc, chunk, g, j)
    bm = bm.reshape(n, nc, chunk, g, -1)
    cm = cm.reshape(n, nc, chunk, g, -1)
    acum = jnp.cumsum(a, axis=2)
    acum_gj = jnp.moveaxis(acum, 2, -1)
    seg = acum_gj[..., :, None] - acum_gj[..., None, :]
    causal = jnp.tril(jnp.ones((chunk, chunk), dtype=bool))
    decay = jnp.exp(jnp.where(causal, seg, -jnp.inf))
    cb = jnp.einsum('nctgk,ncsgk->ncgts', cm, bm)
    w = cb[:, :, :, None] * decay
    y_intra = jnp.einsum('ncgjts,ncsgjp->nctgjp', w, xdt)
    decay_end = jnp.exp(acum[:, :, -1:] - acum)
    s_chunk = jnp.einsum('nclgk,nclgjp->ncgjpk', bm, xdt * decay_end[..., None])
    chunk_decay = jnp.exp(acum[:, :, -1])

    def step(h, inp):
        dec, s = inp
        return h * dec[..., None, None] + s, h

    h_last, h_in = lax.scan(step, h0, (jnp.moveaxis(chunk_decay, 1, 0), jnp.moveaxis(s_chunk, 1, 0)))
    h_in = jnp.moveaxis(h_in, 0, 1)
    y_inter = jnp.einsum('nctgk,ncgjpk->nctgjp', cm, h_in) * jnp.exp(acum)[..., None]
    return (y_intra + y_inter).reshape(n, t_len, g, j, p), h_last


def ssd_mixer(u, hist, h0, w_in, conv_w, conv_b, dt_bias, a_log, d_skip, norm_w, w_out):
    n, t_len, _ = u.shape
    proj = u @ w_in
    z = proj[..., :D_INNER]
    xbc = proj[..., D_INNER:D_INNER + CONV_DIM]
    dt_raw = proj[..., D_INNER + CONV_DIM:]
    xbc_full = jnp.concatenate([hist.astype(xbc.dtype), xbc], axis=1)
    conv_last = xbc_full[:, -(SSM_CONV - 1):]
    xbc = jax.nn.silu(causal_dwconv(xbc_full, conv_w, conv_b).astype(jnp.float32))
    xs = xbc[..., :D_INNER].reshape(n, t_len, SSM_GROUPS, SSM_HPG, SSM_HEAD_DIM)
    bm = xbc[..., D_INNER:D_INNER + SSM_GROUPS * D_STATE].reshape(n, t_len, SSM_GROUPS, D_STATE)
    cm = xbc[..., D_INNER + SSM_GROUPS * D_STATE:].reshape(n, t_len, SSM_GROUPS, D_STATE)
    dt = jax.nn.softplus(dt_raw.astype(jnp.float32) + dt_bias.astype(jnp.float32))
    dt = dt.reshape(n, t_len, SSM_GROUPS, SSM_HPG)
    a = dt * (-jnp.exp(a_log.astype(jnp.float32))).reshape(SSM_GROUPS, SSM_HPG)
    chunk = SSM_CHUNK if t_len % SSM_CHUNK == 0 else t_len
    h0g = h0.astype(jnp.float32).reshape(n, SSM_GROUPS, SSM_HPG, SSM_HEAD_DIM, D_STATE)
    y, h_last = ssd_scan(xs * dt[..., None], a, bm, cm, h0g, chunk)
    y = y + xs * d_skip.astype(jnp.float32).reshape(SSM_GROUPS, SSM_HPG, 1)
    gated = y.reshape(n, t_len, D_INNER) * jax.nn.silu(z.astype(jnp.float32))
    gated = gated.reshape(n, t_len, SSM_GROUPS, D_INNER // SSM_GROUPS)
    gated = gated * lax.rsqrt(jnp.mean(gated * gated, axis=-1, keepdims=True) + EPS)
    out = (gated.reshape(n, t_len, D_INNER) * norm_w.astype(jnp.float32)).astype(u.dtype) @ w_out
    return out, h_last.reshape(n, SSM_HEADS, SSM_HEAD_DIM, D_STATE).astype(h0.dtype), conv_last


def dilated_attn_prompt(q, k, v, window, dil):
    n, t_len, h, e = q.shape
    span = window // dil
    sub = t_len // dil
    nb = -(-sub // span)
    pad = nb * span - sub

    def to_blocks(x):
        x = x.reshape(n, sub, dil, h, e).transpose(0, 2, 1, 3, 4).reshape(n * dil, sub, h, e)
        x = jnp.pad(x, ((0, 0), (0, pad), (0, 0), (0, 0)))
        return x.reshape(n * dil, nb, span, h, e)

    def with_prev(x):
        prev = jnp.concatenate([jnp.zeros_like(x[:, :1]), x[:, :-1]], axis=1)
        return jnp.concatenate([prev, x], axis=2)

    qb = to_blocks(q)
    kk = with_prev(to_blocks(k))
    vv = with_prev(to_blocks(v))
    s = jnp.einsum('rbqhe,rbkhe->rbhqk', qb, kk).astype(jnp.float32) * (e ** -0.5)
    qi = jnp.arange(span)[:, None] + span
    ki = jnp.arange(2 * span)[None, :]
    band = (qi - ki >= 0) & (qi - ki <= span)
    not_first = jnp.arange(nb)[:, None, None] > 0
    valid = band[None] & (not_first | (ki >= span)[None])
    s = jnp.where(valid[None, :, None], s, -jnp.inf)
    m = jnp.max(s, axis=-1, keepdims=True)
    p = jnp.exp(s - m)
    den = jnp.sum(p, axis=-1, keepdims=True)
    o = jnp.einsum('rbhqk,rbkhe->rbqhe', p / den, vv.astype(jnp.float32))
    lse = jnp.moveaxis((m + jnp.log(den))[..., 0], 2, 3)

    def from_blocks(x):
        x = x.reshape(n * dil, nb * span, *x.shape[3:])[:, :sub]
        x = x.reshape(n, dil, sub, *x.shape[2:])
        return jnp.swapaxes(x, 1, 2).reshape(n, t_len, *x.shape[3:])

    return from_blocks(o), from_blocks(lse)


def dilated_attn_sample(q, kv_buf, k_new, v_new, window, dil):
    n, s_len, h, e = q.shape
    wb = kv_buf.shape[1]
    span = window // dil
    k_all = jnp.concatenate([kv_buf[:, :, 0].astype(k_new.dtype), k_new], axis=1)
    v_all = jnp.concatenate([kv_buf[:, :, 1].astype(v_new.dtype), v_new], axis=1)
    idx = wb + jnp.arange(s_len)[:, None] - dil * jnp.arange(span + 1)[None, :]
    valid = idx >= 0
    idx = jnp.maximum(idx, 0)
    kg = k_all[:, idx]
    vg = v_all[:, idx]
    s = jnp.einsum('nshe,nskhe->nhsk', q, kg).astype(jnp.float32) * (e ** -0.5)
    s = jnp.where(valid[None, None], s, -jnp.inf)
    m = jnp.max(s, axis=-1, keepdims=True)
    p = jnp.exp(s - m)
    den = jnp.sum(p, axis=-1, keepdims=True)
    o = jnp.einsum('nhsk,nskhe->nshe', p / den, vg.astype(jnp.float32))
    lse = jnp.moveaxis((m + jnp.log(den))[..., 0], 1, 2)
    return o, lse


def dilated_mixer(u, pos, w_qkv, w_o, bufs):
    n, t_len, _ = u.shape
    qkv = (u @ w_qkv).reshape(n, t_len, 3, ATT_HEADS, ATT_HEAD_DIM)
    q = rope_partial(qkv[:, :, 0], pos)
    k = rope_partial(qkv[:, :, 1], pos)
    v = qkv[:, :, 2]
    outs, lses, new_bufs = [], [], []
    for g, (win, dil) in enumerate(ATT_GROUPS):
        hs = slice(g * ATT_HPG, (g + 1) * ATT_HPG)
        qg, kg, vg = q[:, :, hs], k[:, :, hs], v[:, :, hs]
        if bufs is None:
            o, l = dilated_attn_prompt(qg, kg, vg, win, dil)
            keep = min(win, t_len)
            new_bufs.append(jnp.stack([kg[:, -keep:], vg[:, -keep:]], axis=2))
        else:
            buf = bufs[g]
            o, l = dilated_attn_sample(qg, buf, kg, vg, win, dil)
            wb = buf.shape[1]
            full = jnp.concatenate([buf.astype(kg.dtype), jnp.stack([kg, vg], axis=2)], axis=1)
            new_bufs.append(full[:, -wb:])
        outs.append(o)
        lses.append(l)
    alpha = jax.nn.softmax(jnp.stack(lses, axis=0), axis=0)
    o = jnp.concatenate([alpha[g][..., None] * outs[g] for g in range(len(ATT_GROUPS))], axis=2)
    y = o.reshape(n, t_len, ATT_WIDTH).astype(u.dtype) @ w_o
    return y, new_bufs


def memory_kv(mem, g, w_kv):
    n = mem.shape[0]
    return (rms_norm(mem, g) @ w_kv).reshape(n, N_MEM, 2, MEM_HEADS, MEM_HEAD_DIM)


def cross_attn(u, kv, w_q, w_o):
    n, t_len, _ = u.shape
    q = (u @ w_q).reshape(n, t_len, MEM_HEADS, MEM_HEAD_DIM)
    s = jnp.einsum('nthe,nmhe->nhtm', q, kv[:, :, 0].astype(q.dtype)).astype(jnp.float32) * (MEM_HEAD_DIM ** -0.5)
    p = jax.nn.softmax(s, axis=-1)
    o = jnp.einsum('nhtm,nmhe->nthe', p, kv[:, :, 1].astype(jnp.float32))
    return o.reshape(n, t_len, D_MODEL).astype(u.dtype) @ w_o


def conv_ffn(u, hist, w_gu, conv_w, conv_b, w_down):
    gu = u @ w_gu
    gate, up = gu[..., :D_FF], gu[..., D_FF:]
    g_full = jnp.concatenate([hist.astype(gate.dtype), gate], axis=1)
    new_hist = g_full[:, -(FFN_CONV - 1):]
    gc = causal_dwconv(g_full, conv_w, conv_b)
    hmid = jax.nn.silu(gc.astype(jnp.float32)) * up.astype(jnp.float32)
    return hmid.astype(u.dtype) @ w_down, new_hist


def trunk(x, start, prm, mem, st):
    n, t_len, _ = x.shape
    pos = start + jnp.arange(t_len, dtype=jnp.int32)
    new = {'ssm': [], 'ssm_conv': [], 'swa0': [], 'swa1': [], 'swa2': [], 'mem_kv': [], 'ffn_conv': []}
    for i in range(DEPTH):
        j = i // 2
        g = prm['norms'][i]
        u = rms_norm(x, g[0])
        if i % 2 == 0:
            if st is None:
                hist = jnp.zeros((n, SSM_CONV - 1, CONV_DIM), x.dtype)
                h0 = jnp.zeros((n, SSM_HEADS, SSM_HEAD_DIM, D_STATE), x.dtype)
            else:
                hist, h0 = st['ssm_conv'][j], st['ssm'][j]
            mix, h_last, conv_last = ssd_mixer(u, hist, h0, prm['ssm_w_in'][j], prm['ssm_conv_w'][j],
                                               prm['ssm_conv_b'][j], prm['ssm_dt_bias'][j], prm['ssm_a_log'][j],
                                               prm['ssm_d'][j], prm['ssm_norm_w'][j], prm['ssm_w_out'][j])
            new['ssm'].append(h_last)
            new['ssm_conv'].append(conv_last)
        else:
            bufs = None if st is None else [st['swa%d' % q][j] for q in range(len(ATT_GROUPS))]
            mix, new_bufs = dilated_mixer(u, pos, prm['att_w_qkv'][j], prm['att_w_o'][j], bufs)
            for q in range(len(ATT_GROUPS)):
                new['swa%d' % q].append(new_bufs[q])
        x = x + rms_norm(mix, g[1])
        if st is None:
            kv = memory_kv(mem, prm['mem_norm'][i], prm['xa_w_kv'][i])
            new['mem_kv'].append(kv)
        else:
            kv = st['mem_kv'][i]
        u = rms_norm(x, g[2])
        x = x + rms_norm(cross_attn(u, kv, prm['xa_w_q'][i], prm['xa_w_o'][i]), g[3])
        hist = jnp.zeros((n, FFN_CONV - 1, D_FF), x.dtype) if st is None else st['ffn_conv'][i]
        u = rms_norm(x, g[4])
        f, f_hist = conv_ffn(u, hist, prm['ffn_w_gu'][i], prm['ffn_conv_w'][i], prm['ffn_conv_b'][i],
                             prm['ffn_w_down'][i])
        new['ffn_conv'].append(f_hist)
        x = x + rms_norm(f, g[5])
    return x, {k: jnp.stack(v, axis=0) for k, v in new.items() if v}


def setup_inputs(seed: int = 0) -> dict:
    key = jax.random.key(seed)
    ks = iter(jax.random.split(key, 40))
    f32 = jnp.float32
    n_ssm = (DEPTH + 1) // 2
    n_att = DEPTH // 2

    def nrm(shape, scale=1.0):
        return scale * jax.random.normal(next(ks), shape, f32)

    dt0 = jnp.exp(jax.random.uniform(next(ks), (n_ssm, SSM_HEADS), f32, math.log(1e-3), math.log(1e-1)))
    inp = {}
    inp['x_prompt'] = nrm((BATCH, SEQ, D_MODEL))
    inp['x_sample'] = nrm((DEC_BATCH, DEC_SEQ, D_MODEL))
    inp['mem_prompt'] = nrm((BATCH, N_MEM, D_MODEL))
    inp['state_ssm'] = nrm((n_ssm, DEC_BATCH, SSM_HEADS, SSM_HEAD_DIM, D_STATE), 0.1)
    inp['state_ssm_conv'] = nrm((n_ssm, DEC_BATCH, SSM_CONV - 1, CONV_DIM))
    inp['cache_swa_kv_w128'] = nrm((n_att, DEC_BATCH, min(ATT_GROUPS[0][0], PAST_LEN), 2, ATT_HPG, ATT_HEAD_DIM))
    inp['cache_swa_kv_w512'] = nrm((n_att, DEC_BATCH, min(ATT_GROUPS[1][0], PAST_LEN), 2, ATT_HPG, ATT_HEAD_DIM))
    inp['cache_swa_kv_w2048'] = nrm((n_att, DEC_BATCH, min(ATT_GROUPS[2][0], PAST_LEN), 2, ATT_HPG, ATT_HEAD_DIM))
    inp['cache_mem_kv'] = nrm((DEPTH, DEC_BATCH, N_MEM, 2, MEM_HEADS, MEM_HEAD_DIM))
    inp['state_ffn_conv'] = nrm((DEPTH, DEC_BATCH, FFN_CONV - 1, D_FF))
    inp['norms'] = 1.0 + nrm((DEPTH, 6, D_MODEL), 0.02)
    inp['ssm_w_in'] = nrm((n_ssm, D_MODEL, SSM_IN_DIM), D_MODEL ** -0.5)
    inp['ssm_conv_w'] = nrm((n_ssm, SSM_CONV, CONV_DIM), SSM_CONV ** -0.5)
    inp['ssm_conv_b'] = nrm((n_ssm, CONV_DIM), 0.02)
    inp['ssm_dt_bias'] = dt0 + jnp.log(-jnp.expm1(-dt0))
    inp['ssm_a_log'] = jnp.log(jax.random.uniform(next(ks), (n_ssm, SSM_HEADS), f32, 1.0, 16.0))
    inp['ssm_d'] = 1.0 + nrm((n_ssm, SSM_HEADS), 0.1)
    inp['ssm_norm_w'] = 1.0 + nrm((n_ssm, D_INNER), 0.02)
    inp['ssm_w_out'] = nrm((n_ssm, D_INNER, D_MODEL), D_INNER ** -0.5)
    inp['att_w_qkv'] = nrm((n_att, D_MODEL, 3 * ATT_WIDTH), D_MODEL ** -0.5)
    inp['att_w_o'] = nrm((n_att, ATT_WIDTH, D_MODEL), ATT_WIDTH ** -0.5)
    inp['mem_norm'] = 1.0 + nrm((DEPTH, D_MODEL), 0.02)
    inp['xa_w_q'] = nrm((DEPTH, D_MODEL, D_MODEL), D_MODEL ** -0.5)
    inp['xa_w_kv'] = nrm((DEPTH, D_MODEL, 2 * D_MODEL), D_MODEL ** -0.5)
    inp['xa_w_o'] = nrm((DEPTH, D_MODEL, D_MODEL), D_MODEL ** -0.5)
    inp['ffn_w_gu'] = nrm((DEPTH, D_MODEL, 2 * D_FF), D_MODEL ** -0.5)
    inp['ffn_conv_w'] = nrm((DEPTH, FFN_CONV, D_FF), FFN_CONV ** -0.5)
    inp['ffn_conv_b'] = nrm((DEPTH, D_FF), 0.02)
    inp['ffn_w_down'] = nrm((DEPTH, D_FF, D_MODEL), D_FF ** -0.5)
    return inp


def reference(x_prompt, x_sample, mem_prompt, state_ssm, state_ssm_conv, cache_swa_kv_w128, cache_swa_kv_w512,
              cache_swa_kv_w2048, cache_mem_kv, state_ffn_conv, norms, ssm_w_in, ssm_conv_w, ssm_conv_b,
              ssm_dt_bias, ssm_a_log, ssm_d, ssm_norm_w, ssm_w_out, att_w_qkv, att_w_o, mem_norm, xa_w_q,
              xa_w_kv, xa_w_o, ffn_w_gu, ffn_conv_w, ffn_conv_b, ffn_w_down):
    prm = {'norms': norms, 'ssm_w_in': ssm_w_in, 'ssm_conv_w': ssm_conv_w, 'ssm_conv_b': ssm_conv_b,
           'ssm_dt_bias': ssm_dt_bias, 'ssm_a_log': ssm_a_log, 'ssm_d': ssm_d, 'ssm_norm_w': ssm_norm_w,
           'ssm_w_out': ssm_w_out, 'att_w_qkv': att_w_qkv, 'att_w_o': att_w_o, 'mem_norm': mem_norm,
           'xa_w_q': xa_w_q, 'xa_w_kv': xa_w_kv, 'xa_w_o': xa_w_o, 'ffn_w_gu': ffn_w_gu,
           'ffn_conv_w': ffn_conv_w, 'ffn_conv_b': ffn_conv_b, 'ffn_w_down': ffn_w_down}
    st = {'ssm': state_ssm, 'ssm_conv': state_ssm_conv, 'swa0': cache_swa_kv_w128, 'swa1': cache_swa_kv_w512,
          'swa2': cache_swa_kv_w2048, 'mem_kv': cache_mem_kv, 'ffn_conv': state_ffn_conv}
    y_prompt, sp = trunk(x_prompt, 0, prm, mem_prompt, None)
    y_sample, ss = trunk(x_sample, PAST_LEN, prm, None, st)
    return (y_prompt, y_sample,
            sp['ssm'], sp['ssm_conv'], sp['swa0'], sp['swa1'], sp['swa2'], sp['mem_kv'], sp['ffn_conv'],
            ss['ssm'], ss['ssm_conv'], ss['swa0'], ss['swa1'], ss['swa2'], ss['ffn_conv'])
```

```python
import contextlib
import numpy as np
import concourse.bass as bass
import concourse.mybir as mybir
from concourse.bass_utils import run_bass_kernel_spmd

F32 = mybir.dt.float32
BF16 = mybir.dt.bfloat16
AF = mybir.ActivationFunctionType
ALU = mybir.AluOpType

NCORES = 8
DMA_ROT = 16
EPS = 1e-6


class Buf:
    __slots__ = ("name", "t", "writers", "readers", "psum", "gdeps")

    def __init__(self, name, t=None, psum=False):
        self.name = name
        self.t = t
        self.writers = {}
        self.readers = {}
        self.gdeps = []
        self.psum = psum

    def __getitem__(self, k):
        return self.t[k]


class Op:
    __slots__ = ("eng", "fn", "waits", "signal", "seq", "key", "val", "snap", "dma")


class Sched:
    ENGS = ("pe", "act", "dve", "pool", "sp")

    def __init__(self, nc):
        self.nc = nc
        self.ops = {e: [] for e in self.ENGS}
        self.known = {e: {} for e in self.ENGS}
        self.snap = {e: None for e in self.ENGS}
        self.dma_count = {}
        self.rot_n = {}
        self.rot_last = {}
        self.n_ops = 0

    def _deps(self, reads, writes, nowaw, lane=None):
        deps = []
        for b in reads:
            for o in b.writers.values():
                deps.append((o, True))
            if b.psum:
                for ln, o in b.readers.items():
                    if ln != lane:
                        deps.append((o, False))
        for b in writes:
            for o in b.readers.values():
                deps.append((o, False))
            if b.readers or not nowaw:
                for o in b.writers.values():
                    deps.append((o, False))
            else:
                for o in b.gdeps:
                    deps.append((o, False))
        return deps

    def _register(self, op, lane, reads, writes, nowaw):
        for b in writes:
            if b.readers or not nowaw:
                b.gdeps = list(b.readers.values()) + list(b.writers.values())
                b.writers = {lane: op}
                b.readers = {}
            else:
                b.writers[lane] = op
        for b in reads:
            b.readers[lane] = op

    def _place(self, op, eng, deps):
        known = self.known[eng]
        waits = {}
        changed = False
        for (o, raw) in deps:
            if not o.dma and o.eng == eng and eng == "pe" and raw != "force":
                continue
            if known.get(o.key, 0) >= o.val:
                continue
            waits[o.key] = max(waits.get(o.key, 0), o.val)
            o.signal = True
            if o.snap is not None:
                for k, v in o.snap.items():
                    if known.get(k, 0) < v:
                        known[k] = v
            if known.get(o.key, 0) < o.val:
                known[o.key] = o.val
            changed = True
        if changed or self.snap[eng] is None:
            self.snap[eng] = dict(known)
        op.snap = self.snap[eng]
        op.waits = list(waits.items())
        self.ops[eng].append(op)
        self.n_ops += 1

    def add(self, eng, fn, reads=(), writes=(), nowaw=True, force=()):
        op = Op()
        op.eng = eng
        op.fn = fn
        op.signal = False
        op.dma = False
        op.seq = len(self.ops[eng]) + 1
        op.key = eng
        op.val = op.seq
        deps = self._deps(reads, writes, nowaw, eng)
        for o in force:
            deps.append((o, "force"))
        self._place(op, eng, deps)
        self._register(op, eng, reads, writes, nowaw)
        return op

    def dma(self, queue, out, in_, reads=(), writes=(), key=None, nowaw=True, **kw):
        op = Op()
        op.eng = queue
        op.dma = True
        op.signal = True
        op.seq = len(self.ops[queue]) + 1
        deps = self._deps(reads, writes, nowaw)
        if isinstance(key, str):
            kname = ("dma", key)
            cnt = self.dma_count.get(kname, 0) + 1
        else:
            n = self.rot_n.get(queue, 0)
            self.rot_n[queue] = n + 1
            kname = ("dma", queue, n % DMA_ROT)
            cnt = n // DMA_ROT + 1
            prev = self.rot_last.get(kname)
            if prev is not None:
                deps.append((prev, False))
            self.rot_last[kname] = op
        self.dma_count[kname] = cnt
        op.key = kname
        op.val = cnt
        op.fn = lambda e: e.dma_start(out=out, in_=in_, **kw)
        self._place(op, queue, deps)
        self._register(op, kname, reads, writes, nowaw)
        return op

    def finish(self):
        op = Op()
        op.eng = "sp"
        op.fn = None
        op.signal = False
        op.dma = False
        op.seq = len(self.ops["sp"]) + 1
        op.key = "sp"
        op.val = op.seq
        op.snap = None
        op.waits = [(k, c) for k, c in self.dma_count.items()]
        self.ops["sp"].append(op)

    def emit(self):
        nc = self.nc
        with contextlib.ExitStack() as es:
            sems = {}
            for e in ("pe", "act", "dve", "pool"):
                sems[e] = es.enter_context(nc.semaphore("s_" + e))
            for i, k in enumerate(self.dma_count):
                sems[k] = es.enter_context(nc.semaphore("d%d" % i))
            sigval = {}
            for e in ("pe", "act", "dve", "pool"):
                c = 0
                tab = {}
                for op in self.ops[e]:
                    if not op.dma and op.signal:
                        c += 1
                        tab[op.seq] = c
                sigval[e] = tab
            block = es.enter_context(nc.Block())

            def run(ename, eng):
                for op in self.ops[ename]:
                    for (k, v) in op.waits:
                        if isinstance(k, tuple):
                            eng.wait_ge(sems[k], 16 * v)
                        else:
                            eng.wait_ge(sems[k], sigval[k][v])
                    if op.fn is None:
                        continue
                    ins = op.fn(eng)
                    if op.dma:
                        ins.then_inc(sems[op.key], 16)
                    elif op.signal:
                        ins.then_inc(sems[ename], 1)

            @block.tensor
            def _(e):
                run("pe", e)

            @block.scalar
            def _(e):
                run("act", e)

            @block.vector
            def _(e):
                run("dve", e)

            @block.gpsimd
            def _(e):
                run("pool", e)

            @block.sync
            def _(e):
                run("sp", e)


class Ring:
    def __init__(self, bufs):
        self.bufs = bufs
        self.i = 0

    def get(self):
        b = self.bufs[self.i % len(self.bufs)]
        self.i += 1
        return b


D = 1024
TP = 2048
NS = 4
T = TP + NS
DI = 2048
NH = 32
NFF = 22
DFF = 2816
CONV = 3072
INW = 5152
DEPTH = 4
TW = 516
SLOT = 4096
NSLOT = 4
ARENA_BYTES = 36 * 1024
MIXERS_ON = True
import os
DBG = int(os.environ.get('KDBG', '99'))
DBG2 = int(os.environ.get('KDBG2', '99'))
ADBG = int(os.environ.get('ADBG', '99'))
HOIST = int(os.environ.get('HOIST', '1'))
BGON = int(os.environ.get('BGON', '0'))
DEFER = int(os.environ.get('DEFER', '1'))

IN_SPECS = [
    ("xp", [TP, D]), ("xs", [NS, D]), ("mem", [256, D]),
    ("st_ssm", [2, NS, 32 * 64, 128]), ("st_conv", [2, NS, 3, CONV]),
    ("c128", [2, NS, 128, 512]), ("c512", [2, NS, 512, 512]), ("c2048", [2, NS, 2048, 512]),
    ("cmem", [4, NS, 256, 2048]), ("st_ffn", [4, NS, 2, DFF]),
    ("norms", [24, D]), ("ssm_w_in", [2, D, INW]), ("ssm_conv_w", [8, CONV]), ("ssm_conv_b", [2, CONV]),
    ("ssm_dt_bias", [2, 32]), ("ssm_a_log", [2, 32]), ("ssm_d", [2, 32]), ("ssm_norm_w", [2, DI]),
    ("ssm_w_out", [2, DI, D]), ("att_w_qkv", [2, D, 2304]), ("att_w_o", [2, 768, D]), ("mem_norm", [4, D]),
    ("xa_w_q", [4, D, D]), ("xa_w_kv", [4, D, 2 * D]), ("xa_w_o", [4, D, D]), ("ffn_w_gu", [4, D, 2 * DFF]),
    ("ffn_conv_w", [12, DFF]), ("ffn_conv_b", [4, DFF]), ("ffn_w_down", [4, DFF, D]), ("rope", [T, 16]),
]
OUT_SPECS = [
    ("y_p", [TP, D]), ("y_s", [NS, D]), ("o_pssm", [2, 2048, 128]), ("o_pconv", [2, 3, CONV]),
    ("o_p128", [2, 128, 512]), ("o_p512", [2, 512, 512]), ("o_p2048", [2, 2048, 512]),
    ("o_pmem", [4, 256, 2048]), ("o_pffn", [4, 2, DFF]),
    ("o_sssm", [2, NS, 2048, 128]), ("o_sconv", [2, NS, 3, CONV]),
    ("o_s128", [2, NS, 128, 512]), ("o_s512", [2, NS, 512, 512]), ("o_s2048", [2, NS, 2048, 512]),
    ("o_sffn", [4, NS, 2, DFF]),
]
TILES = [[(0, 512, 0)], [(512, 512, 0)], [(1024, 512, 0)], [(1536, 512, 0), (2048, 4, 512)]]


def build(stop_after=None):
    nc = bass.Bass("TRN2", target_bir_lowering=False)
    S = Sched(nc)
    I = {n: nc.dram_tensor(n, s, F32, kind="ExternalInput").ap() for n, s in IN_SPECS}
    O = {n: nc.dram_tensor(n, s, F32, kind="ExternalOutput").ap() for n, s in OUT_SPECS}
    vscr = nc.dram_tensor("vscr", [T, 768], F32, kind="Internal").ap()
    VSCR = Buf("vscr")
    es = contextlib.ExitStack()
    cnt = [0]

    def sb(name, shape, dt):
        return Buf(name, es.enter_context(nc.sbuf_tensor(name, shape, dt)))

    def ring(name, n, shape, dt):
        return Ring([sb("%s%d" % (name, k), shape, dt) for k in range(n)])

    last_rg = {}

    def mm(out, lhsT, rhs, start, stop, reads, wbuf, rg=0):
        prev = last_rg.get(wbuf.name)
        force = [prev[0]] if (prev is not None and prev[1] != rg) else []
        op = S.add("pe", lambda e: e.matmul(out, lhsT=lhsT, rhs=rhs, start=start, stop=stop), reads, [wbuf], force=force)
        last_rg[wbuf.name] = (op, rg)

    def trp(out, in_, ident, reads, wbuf):
        S.add("pe", lambda e: e.transpose(out=out, in_=in_, identity=ident), reads, [wbuf])

    def act(out, in_, func, reads, writes, **kw):
        S.add("act", lambda e: e.activation(out=out, in_=in_, func=func, **kw), reads, writes)

    def tt(eng, out, in0, in1, op, reads, writes):
        S.add(eng, lambda e: e.tensor_tensor(out=out, in0=in0, in1=in1, op=op), reads, writes)

    def stt(out, in0, scalar, in1, op0, op1, reads, writes):
        S.add("dve", lambda e: e.scalar_tensor_tensor(out=out, in0=in0, scalar=scalar, in1=in1, op0=op0, op1=op1),
              reads, writes)

    def tsc(eng, out, in0, s1, s2, op0, op1, reads, writes):
        if s2 is None:
            S.add(eng, lambda e: e.tensor_scalar(out=out, in0=in0, scalar1=s1, scalar2=None, op0=op0), reads, writes)
        else:
            S.add(eng, lambda e: e.tensor_scalar(out=out, in0=in0, scalar1=s1, scalar2=s2, op0=op0, op1=op1),
                  reads, writes)

    def cp(eng, out, in_, reads, writes):
        if eng == "act":
            S.add("act", lambda e: e.activation(out=out, in_=in_, func=AF.Copy), reads, writes)
        else:
            S.add(eng, lambda e: e.tensor_copy(out=out, in_=in_), reads, writes)

    def recip(out, in_, reads, writes):
        S.add("dve", lambda e: e.reciprocal(out=out, in_=in_), reads, writes)

    def mset(eng, out, val, writes):
        S.add(eng, lambda e: e.memset(out, val), (), writes, nowaw=False)

    xT = es.enter_context(nc.sbuf_tensor("xT", [128, 8, T], F32))
    XT = [Buf("xT%d" % k, xT) for k in range(4)]
    slots = [sb("wslot%d" % k, [128, SLOT], BF16) for k in range(NSLOT)]
    URING = ring("U", 1, [128, 8, TW], BF16)
    b24 = es.enter_context(nc.sbuf_tensor("b24", [128, 24 * TW], BF16))
    B24 = [Buf("b24_%d" % k, b24) for k in range(3)]
    FB = sb("fsb", [128, 8, TW], BF16)
    PSM = Ring([Buf("psm%d" % k, es.enter_context(nc.psum_tensor("psm%d" % k, [128, 512], F32)), psum=True) for k in range(4)])
    PSX = Ring([Buf("psx%d" % k, es.enter_context(nc.psum_tensor("psx%d" % k, [128, 512], F32)), psum=True) for k in range(2)])
    PST = Ring([Buf("pst%d" % k, es.enter_context(nc.psum_tensor("pst%d" % k, [128, 512], F32)), psum=True) for k in range(2)])
    RS = ring("rs", 3, [128, TW + 4], F32)
    TMP = ring("tmp", 6, [128, TW + 4], F32)
    BT = ring("bt", 3, [128, TW + 4], BF16)
    SM = ring("sm", 8, [128, 64], F32)
    STG = TMP
    RPOST = ring("rpost", 2, [128, TW + 4], F32) if BGON else None

    arena = es.enter_context(nc.sbuf_tensor("arena", [128, ARENA_BYTES // 4], F32))
    ast = {"off": 0, "bufs": [], "fence": None}

    def aalloc(name, shape, dt):
        n = 1
        for d_ in shape[1:]:
            n *= d_
        nbytes = n * (4 if dt == F32 else 2)
        nbytes = (nbytes + 63) // 64 * 64
        off = ast["off"]
        assert off + nbytes <= ARENA_BYTES, (name, off, nbytes)
        ast["off"] = off + nbytes
        ap = arena[:, off // 4:(off + nbytes) // 4]
        if dt == BF16:
            ap = ap.bitcast(BF16)
        ap = ap[:, 0:n]
        if len(shape) == 3:
            ap = ap.rearrange("p (a b) -> p a b", a=shape[1])
        elif len(shape) == 4:
            ap = ap.rearrange("p (a b c) -> p a b c", a=shape[1], b=shape[2])
        b = Buf("A_" + name, ap)
        if ast["fence"] is not None:
            b.readers = {"pool": ast["fence"]}
            b.gdeps = [ast["fence"]]
        ast["bufs"].append(b)
        return b

    def areset():
        if ast["bufs"]:
            dm = SM.get()
            ast["fence"] = S.add("pool", lambda e: e.memset(dm[:, 0:1], 0.0), [], ast["bufs"] + [dm], nowaw=False)
        ast["off"] = 0
        ast["bufs"] = []

    identf = sb("identf", [128, 128], F32)
    identb = sb("identb", [128, 128], BF16)
    onesf = sb("onesf", [128, 128], F32)
    onesb = sb("onesb", [128, 128], BF16)
    trile_f = sb("trile_f", [128, 128], F32)
    trile_b = sb("trile_b", [128, 128], BF16)
    trige_b = sb("trige_b", [128, 128], BF16)
    maskgt_f = sb("maskgt_f", [128, 128], F32)
    neghalf = sb("neghalf", [128, 1], F32)
    GT = sb("gT", [128, 8, 24], F32)
    SCW = sb("scw", [128, 24, 10], F32)
    SNW = sb("snw", [128, 16, 2], F32)
    FCW = sb("fcw", [128, NFF, 16], F32)
    ropet = sb("ropet", [128, 17, 16], F32)

    mset("pool", onesf[:], 1.0, [onesf])
    mset("pool", neghalf[:], -0.5, [neghalf])
    S.add("pool", lambda e: e.affine_select(out=identf[:], in_=onesf[:], pattern=[[-1, 128]], compare_op=ALU.is_equal,
                                            fill=0.0, base=0, channel_multiplier=1), [onesf], [identf], nowaw=False)
    S.add("pool", lambda e: e.affine_select(out=trile_f[:], in_=onesf[:], pattern=[[1, 128]], compare_op=ALU.is_ge,
                                            fill=0.0, base=0, channel_multiplier=-1), [onesf], [trile_f], nowaw=False)
    S.add("pool", lambda e: e.affine_select(out=maskgt_f[:], in_=onesf[:], pattern=[[-1, 128]], compare_op=ALU.is_gt,
                                            fill=0.0, base=0, channel_multiplier=1), [onesf], [maskgt_f], nowaw=False)
    S.add("pool", lambda e: e.affine_select(out=trige_b[:], in_=onesf[:], pattern=[[-1, 128]], compare_op=ALU.is_ge,
                                            fill=0.0, base=0, channel_multiplier=1), [onesf], [trige_b], nowaw=False)
    cp("dve", identb[:], identf[:], [identf], [identb])
    cp("dve", onesb[:], onesf[:], [onesf], [onesb])
    cp("dve", trile_b[:], trile_f[:], [trile_f], [trile_b])
    S.dma("sp", ropet[:, 0:16, :], I["rope"][0:TP, :].rearrange("(c p) f -> p c f", p=128), writes=[ropet])
    S.dma("sp", ropet[0:NS, 16, :], I["rope"][TP:T, :], writes=[ropet])

    def load_featT(dst, srcs, F):
        R = sum(r for _, r in srcs)
        nj = F // 128
        for b in range((nj + 3) // 4):
            st_ = STG.get()
            j0 = b * 4
            jn = min(4, nj - j0)
            r0 = 0
            for ap, r in srcs:
                S.dma("sp", st_[r0:r0 + r, 0:jn * 128], ap[:, j0 * 128:(j0 + jn) * 128], writes=[st_])
                r0 += r
            ps = PST.get()
            for q in range(jn):
                trp(ps[:, q * R:(q + 1) * R], st_[0:R, q * 128:(q + 1) * 128], identf[0:R, 0:R], [st_, identf], ps)
            cp("dve", dst[:, j0:j0 + jn, :], ps[:, 0:jn * R].rearrange("p (g r) -> p g r", r=R), [ps], [dst])

    load_featT(GT, [(I["norms"], 24)], D)
    load_featT(SCW, [(I["ssm_conv_w"], 8), (I["ssm_conv_b"], 2)], CONV)
    load_featT(SNW, [(I["ssm_norm_w"], 2)], DI)
    load_featT(FCW, [(I["ffn_conv_w"], 12), (I["ffn_conv_b"], 4)], DFF)
    tsc("dve", SCW[:], SCW[:], 0.5, None, ALU.mult, None, [SCW], [SCW])
    tsc("dve", FCW[:], FCW[:], 0.5, None, ALU.mult, None, [FCW], [FCW])

    def rows_out(src_fn, rbufs, R, nj, dst):
        for b in range((nj + 3) // 4):
            j0 = b * 4
            jn = min(4, nj - j0)
            st_ = STG.get()
            ps = PST.get()
            for q in range(jn):
                trp(ps[0:R, q * 128:(q + 1) * 128], src_fn(j0 + q), identf[:], rbufs + [identf], ps)
            cp("dve", st_[0:R, 0:jn * 128], ps[0:R, 0:jn * 128], [ps], [st_])
            S.dma("sp", dst[:, j0 * 128:(j0 + jn) * 128], st_[0:R, 0:jn * 128], reads=[st_], key=st_)

    for tb in range(16):
        for hh in range(2):
            st_ = STG.get()
            S.dma("sp", st_[:, 0:512], I["xp"][tb * 128:(tb + 1) * 128, hh * 512:(hh + 1) * 512], writes=[st_])
            ps = PSM.get()
            for q in range(4):
                trp(ps[:, q * 128:(q + 1) * 128], st_[:, q * 128:(q + 1) * 128], identf[:], [st_, identf], ps)
            cp("act" if hh else "dve", xT[:, hh * 4:hh * 4 + 4, tb * 128:(tb + 1) * 128],
               ps[:, :].rearrange("p (k t) -> p k t", k=4), [ps], [XT[tb // 4]])
    for hh in range(2):
        st_ = STG.get()
        S.dma("sp", st_[0:NS, 0:512], I["xs"][:, hh * 512:(hh + 1) * 512], writes=[st_])
        ps = PST.get()
        for q in range(4):
            trp(ps[:, q * NS:(q + 1) * NS], st_[0:NS, q * 128:(q + 1) * 128], identf[0:NS, 0:NS], [st_, identf], ps)
        cp("dve", xT[:, hh * 4:hh * 4 + 4, TP:T], ps[:, 0:4 * NS].rearrange("p (k t) -> p k t", k=4), [ps], [XT[3]])

    plan = []

    def P_(tag, parts):
        plan.append((tag, parts))

    def kview(n_k, ncols, off=0):
        return lambda s: s[:, off:off + n_k * ncols].rearrange("p (k n) -> p k n", k=n_k)

    def wsrc(ap2d, r0, nk, c0, n):
        return ap2d[r0:r0 + nk * 128, c0:c0 + n].rearrange("(k p) n -> p k n", p=128)

    def plan_mix(i):
        j = i // 2
        for ti in range(4):
            if i % 2 == 0:
                w_in = I["ssm_w_in"][j]
                for s_ in range(6):
                    P_(("xbc", i, ti, s_), [(kview(8, 512), wsrc(w_in, 0, 8, DI + s_ * 512, 512))])
                for g in range(4):
                    P_(("z", i, ti, g), [(kview(8, 512), wsrc(w_in, 0, 8, g * 512, 512))])
                for s_ in range(4):
                    P_(("wout", i, ti, s_), [(kview(16, 256), wsrc(I["ssm_w_out"][j], 0, 16, s_ * 256, 256))])
            else:
                wq = I["att_w_qkv"][j]
                for s_ in range(6):
                    P_(("qkv", i, ti, s_), [(kview(8, 384), wsrc(wq, 0, 8, s_ * 384, 384))])
                for s_ in range(4):
                    P_(("wo", i, ti, s_), [(lambda s: s[0:64, 0:12 * 256].rearrange("p (k n) -> p k n", k=12),
                                            I["att_w_o"][j][:, s_ * 256:(s_ + 1) * 256].rearrange("(k p) n -> p k n", p=64))])

    def plan_xa(i):
        for q in range(4):
            P_(("kv", i, q), [(kview(8, 512), wsrc(I["xa_w_kv"][i], 0, 8, q * 512, 512))])
        for ti in range(4):
            for q in range(2):
                P_(("xq", i, ti, q), [(kview(8, 512), wsrc(I["xa_w_q"][i], 0, 8, q * 512, 512))])
            for q in range(2):
                P_(("xo", i, ti, q), [(kview(8, 512), wsrc(I["xa_w_o"][i], 0, 8, q * 512, 512))])

    def plan_ffn(i):
        gu = I["ffn_w_gu"][i]
        for ti in range(4):
            for s_ in range(11):
                P_(("gu", i, ti, s_), [(kview(8, 256, 0), wsrc(gu, 0, 8, s_ * 256, 256)),
                                       (kview(8, 256, 2048), wsrc(gu, 0, 8, DFF + s_ * 256, 256))])
            for mp in range(4):
                for hf in range(2):
                    P_(("dn", i, ti, mp, hf), [(kview(11, 256), wsrc(I["ffn_w_down"][i], hf * 11 * 128, 11, mp * 256, 256))])

    stages = []
    for i in range(DEPTH):
        stages += [("mix", i), ("xa", i), ("ffn", i)]
    if stop_after is not None:
        stages = stages[:stop_after]
    stages = [s for s in stages if s[0] != "mix" or MIXERS_ON]
    for kind, i in stages:
        {"mix": plan_mix, "xa": plan_xa, "ffn": plan_ffn}[kind](i)

    wst = {"issued": 0, "taken": 0}

    def take(tag):
        idx = wst["taken"]
        assert plan[idx][0] == tag, (plan[idx][0], tag)
        while wst["issued"] < min(len(plan), idx + NSLOT - 1):
            k = wst["issued"]
            sl = slots[k % NSLOT]
            for vf, src in plan[k][1]:
                S.dma("pool", vf(sl.t), src, writes=[sl])
            wst["issued"] += 1
        wst["taken"] += 1
        return slots[idx % NSLOT]

    def rstd_from(ps, w):
        v = RS.get()
        act(v[:, :w], ps[:, :w], AF.Ln, [ps], [v], scale=1.0 / D, bias=EPS)
        r = RS.get()
        act(r[:, :w], v[:, :w], AF.Exp, [v], [r], scale=-0.5)
        return r

    def prenorm(ti, gi):
        bg_drain()
        U = URING.get()
        for st in TILES[ti]:
            c0, w, lc = st
            ps = PSX.get()
            for kc in range(8):
                sq = BT.get()
                act(sq[:, :w], xT[:, kc, c0:c0 + w], AF.Square, [XT[ti]], [sq])
                mm(ps[:, :w], onesb[:], sq[:, :w], kc == 0, kc == 7, [onesb, sq], ps)
            r = rstd_from(ps, w)
            for kc in range(8):
                stt(U[:, kc, lc:lc + w], xT[:, kc, c0:c0 + w], GT[:, kc, gi:gi + 1], r[:, :w], ALU.mult, ALU.mult,
                    [XT[ti], GT, r], [U])
        return U

    def post_begin(ti):
        bg_drain()
        return {st: PSX.get() for st in TILES[ti]}

    pend = []

    def post_flush():
        while pend:
            sq, mc, st, SQ, w = pend.pop(0)
            mm(SQ[st][:, :w], onesb[:], sq[:, :w], mc == 0, mc == 7, [onesb, sq], SQ[st])

    def post_evac(ps, mc, st, SQ):
        c0, w, lc = st
        act(FB[:, mc, lc:lc + w], ps[:, :w], AF.Copy, [ps], [FB])
        sq = BT.get()
        act(sq[:, :w], ps[:, :w], AF.Square, [ps], [sq])
        post_flush()
        pend.append((sq, mc, st, SQ, w))
        if not DEFER:
            post_flush()

    bgq = []

    def bg_step(n=1):
        for _ in range(n):
            if bgq:
                bgq.pop(0)()

    def bg_drain():
        while bgq:
            bgq.pop(0)()

    def post_finish(ti, gi, SQ):
        post_flush()
        bg_drain()
        for st in TILES[ti]:
            c0, w, lc = st
            v = RS.get()
            act(v[:, :w], SQ[st][:, :w], AF.Ln, [SQ[st]], [v], scale=1.0 / D, bias=EPS)
            r = RPOST.get() if BGON else RS.get()
            act(r[:, :w], v[:, :w], AF.Exp, [v], [r], scale=-0.5)
            for mc in range(8):
                def upd(mc=mc, c0=c0, w=w, lc=lc, r=r):
                    tm = TMP.get()
                    stt(tm[:, :w], FB[:, mc, lc:lc + w], GT[:, mc, gi:gi + 1], r[:, :w], ALU.mult, ALU.mult,
                        [FB, GT, r], [tm])
                    tt("pool", xT[:, mc, c0:c0 + w], xT[:, mc, c0:c0 + w], tm[:, :w], ALU.add, [XT[ti], tm], [XT[ti]])
                if BGON and ti < 3:
                    bgq.append(upd)
                else:
                    upd()

    RAW = ACC = TH = TMP
    hv = b24[:, :].rearrange("p (k n) -> p k n", k=24)

    def hbuf(jc):
        return B24[jc // 8]

    def ffn(i):
        areset()
        GH = aalloc("gh", [128, NFF, 2], F32)
        SGH = aalloc("sgh", [128, NFF, 8], F32)
        GNEW = aalloc("gnew", [128, NFF, NS], F32)
        mset("pool", GH[:], 0.0, [GH])
        for b in range(6):
            st_ = STG.get()
            j0 = b * 4
            jn = min(4, NFF - j0)
            S.dma("sp", st_[0:8, 0:jn * 128], I["st_ffn"][i].rearrange("n r f -> (n r) f")[:, j0 * 128:(j0 + jn) * 128],
                  writes=[st_])
            ps = PST.get()
            for q in range(jn):
                trp(ps[:, q * 8:(q + 1) * 8], st_[0:8, q * 128:(q + 1) * 128], identf[0:8, 0:8], [st_, identf], ps)
            cp("dve", SGH[:, j0:j0 + jn, :], ps[:, 0:jn * 8].rearrange("p (g r) -> p g r", r=8), [ps], [SGH])
        S.dma("sp", O["o_sffn"][i][:, 0, :], I["st_ffn"][i][:, 1, :], key="outc")
        for ti in range(4):
            U = Unext if (ti > 0 and HOIST) else prenorm(ti, i * 6 + 4)
            for s_ in range(11):
                sl = take(("gu", i, ti, s_))
                sv = sl[:, :].rearrange("p (a k n) -> p a k n", a=2, k=8)
                for jj in range(2):
                    bg_step()
                    jc = 2 * s_ + jj
                    w0 = FCW[:, jc, i * 3 + 0:i * 3 + 1]
                    w1 = FCW[:, jc, i * 3 + 1:i * 3 + 2]
                    w2 = FCW[:, jc, i * 3 + 2:i * 3 + 3]
                    bb = FCW[:, jc, 12 + i:13 + i]
                    for st in TILES[ti]:
                        c0, w, lc = st
                        pg = PSM.get()
                        for kc in range(8):
                            mm(pg[:, :w], sv[:, 0, kc, jj * 128:(jj + 1) * 128], U[:, kc, lc:lc + w], kc == 0, kc == 7,
                               [sl, U], pg)
                        pu = PSM.get()
                        for kc in range(8):
                            mm(pu[:, :w], sv[:, 1, kc, jj * 128:(jj + 1) * 128], U[:, kc, lc:lc + w], kc == 0, kc == 7,
                               [sl, U], pu)
                        ac = ACC.get()
                        if w > NS:
                            raw = RAW.get()
                            cp("pool", raw[:, 0:2], GH[:, jc, :], [GH], [raw])
                            act(raw[:, 2:2 + w], pg[:, :w], AF.Copy, [pg], [raw])
                            cp("pool", GH[:, jc, :], raw[:, w:w + 2], [raw], [GH])
                            act(ac[:, :w], raw[:, 0:w], AF.Identity, [raw, FCW], [ac], scale=w0, bias=bb)
                            stt(ac[:, :w], raw[:, 1:1 + w], w1, ac[:, :w], ALU.mult, ALU.add, [raw, FCW, ac], [ac])
                            stt(ac[:, :w], pg[:, :w], w2, ac[:, :w], ALU.mult, ALU.add, [pg, FCW, ac], [ac])
                        else:
                            act(ac[:, :w], SGH[:, jc, 0:8:2], AF.Identity, [SGH, FCW], [ac], scale=w0, bias=bb)
                            stt(ac[:, :w], SGH[:, jc, 1:8:2], w1, ac[:, :w], ALU.mult, ALU.add, [SGH, FCW, ac], [ac])
                            stt(ac[:, :w], pg[:, :w], w2, ac[:, :w], ALU.mult, ALU.add, [pg, FCW, ac], [ac])
                            act(GNEW[:, jc, :], pg[:, :w], AF.Copy, [pg], [GNEW])
                        th = TH.get()
                        act(th[:, :w], ac[:, :w], AF.Tanh, [ac], [th])
                        stt(ac[:, :w], th[:, :w], 1.0, ac[:, :w], ALU.add, ALU.mult, [th, ac], [ac])
                        tt("dve", hv[:, jc, lc:lc + w], ac[:, :w], pu[:, :w], ALU.mult, [ac, pu], [hbuf(jc)])
            if ti == 3:
                rows_out(lambda jc: GH[:, jc, :], [GH], 2, NFF, O["o_pffn"][i])
                rows_out(lambda jc: GNEW[:, jc, :], [GNEW], NS, NFF, O["o_sffn"][i][:, 1, :])
            Unext = prenorm(ti + 1, i * 6 + 4) if (ti < 3 and HOIST) else None
            SQ = post_begin(ti)
            for mp in range(4):
                sa = take(("dn", i, ti, mp, 0))
                sb_ = take(("dn", i, ti, mp, 1))
                va = sa[:, 0:11 * 256].rearrange("p (k n) -> p k n", k=11)
                vb = sb_[:, 0:11 * 256].rearrange("p (k n) -> p k n", k=11)
                for m2 in range(2):
                    mc = 2 * mp + m2
                    for st in TILES[ti]:
                        c0, w, lc = st
                        pf = PSM.get()
                        for jc in range(NFF):
                            src = (va if jc < 11 else vb)[:, jc % 11, m2 * 128:(m2 + 1) * 128]
                            mm(pf[:, :w], src, hv[:, jc, lc:lc + w], jc == 0, jc == NFF - 1,
                               [sa if jc < 11 else sb_, hbuf(jc)], pf)
                        post_evac(pf, mc, st, SQ)
            post_finish(ti, i * 6 + 5, SQ)

    qv = b24[:, 0:8 * TW].rearrange("p (k n) -> p k n", k=8)
    ov = b24[:, 8 * TW:16 * TW].rearrange("p (k n) -> p k n", k=8)

    def make_kt(dst, src):
        for h4 in range(2):
            ps = PST.get()
            pb = ps[:, :].bitcast(BF16)
            for q in range(4):
                hc = h4 * 4 + q
                for mb in range(2):
                    o0 = (q * 2 + mb) * 128
                    trp(pb[:, o0:o0 + 128], src[:, mb, hc * 128:(hc + 1) * 128], identb[:], [src, identb], ps)
            cp("act", dst[:, h4 * 4:h4 * 4 + 4, :], pb[:, :].rearrange("p (q m) -> p q m", q=4), [ps], [dst])

    def xattn_a(PTR, ktb, h, w, lc):
        PTb = PTR.get()
        for mb in range(2):
            ps = PSM.get()
            for ec in range(2):
                mm(ps[:, :w], ktb[:, h * 2 + ec, mb * 128:(mb + 1) * 128], qv[:, h * 2 + ec, lc:lc + w], ec == 0, ec == 1,
                   [ktb, B24[0]], ps)
            act(PTb[:, mb, :w], ps[:, :w], AF.Exp, [ps], [PTb])
        return PTb

    def xattn_b(PTb, vbb, h, w, lc):
        pd = PSX.get()
        for mb in range(2):
            mm(pd[:, :w], onesb[:], PTb[:, mb, :w], mb == 0, mb == 1, [onesb, PTb], pd)
        rd = RS.get()
        act(rd[:, :w], pd[:, :w], AF.Ln, [pd], [rd])
        act(rd[:, :w], rd[:, :w], AF.Exp, [rd], [rd], scale=-1.0)
        for ec in range(2):
            po = PSM.get()
            for mb in range(2):
                mm(po[:, :w], vbb[:, mb, h * 256 + ec * 128:h * 256 + ec * 128 + 128], PTb[:, mb, :w], mb == 0, mb == 1,
                   [vbb, PTb], po)
            tt("dve", ov[:, h * 2 + ec, lc:lc + w], po[:, :w], rd[:, :w], ALU.mult, [po, rd], [B24[1]])

    def xattn_core(PTR, ktb, vbb, h, w, lc):
        xattn_b(xattn_a(PTR, ktb, h, w, lc), vbb, h, w, lc)

    def xattn(i):
        areset()
        MNT = aalloc("mnt", [128, 8, 256], BF16)
        KB = aalloc("kb", [128, 2, 1024], BF16)
        VB = aalloc("vb", [128, 2, 1024], BF16)
        KT = aalloc("kt", [128, 8, 256], BF16)
        SVB = aalloc("svb", [128, 2, 1024], BF16)
        SKT = MNT
        MNB = aalloc("mnb", [128, 1024], BF16)
        mg = aalloc("mg", [128, 1024], F32)
        m_ = aalloc("memblk", [128, 1024], F32)
        PTR = Ring([aalloc("ptr%d" % k, [128, 2, 512], BF16) for k in range(2)])
        S.dma("sp", mg[:, :], I["mem_norm"][i:i + 1, :].partition_broadcast(128), writes=[mg])
        for mb in range(2):
            S.dma("sp", m_[:, :], I["mem"][mb * 128:(mb + 1) * 128, :], writes=[m_])
            ss = SM.get()
            act(MNB[:, :], m_[:, :], AF.Square, [m_], [MNB, ss], accum_out=ss[:, 0:1])
            act(ss[:, 1:2], ss[:, 0:1], AF.Sqrt, [ss], [ss], scale=1.0 / D, bias=EPS)
            recip(ss[:, 2:3], ss[:, 1:2], [ss], [ss])
            S.add("dve", lambda e, ss=ss: e.scalar_tensor_tensor(out=MNB[:, :], in0=m_[:, :], scalar=ss[:, 2:3], in1=mg[:, :],
                                                                       op0=ALU.mult, op1=ALU.mult), [m_, ss, mg], [MNB], nowaw=False)
            ps = PST.get()
            pb = ps[:, :].bitcast(BF16)
            for kc in range(8):
                trp(pb[:, kc * 128:(kc + 1) * 128], MNB[:, kc * 128:(kc + 1) * 128], identb[:], [MNB, identb], ps)
            cp("act", MNT[:, :, mb * 128:(mb + 1) * 128], pb[:, :].rearrange("p (k m) -> p k m", k=8), [ps], [MNT])
        if DBG < 1:
            return
        for q in range(4):
            sl = take(("kv", i, q))
            sv = sl[:, :].rearrange("p (k n) -> p k n", k=8)
            for mb in range(2):
                if DBG2 < 1:
                    continue
                ps = PSM.get()
                for kc in range(8):
                    mm(ps[:, :], MNT[:, kc, mb * 128:(mb + 1) * 128], sv[:, kc, :], kc == 0, kc == 7, [MNT, sl], ps)
                if DBG2 < 2:
                    continue
                st_ = STG.get()
                act(st_[:, 0:512], ps[:, :], AF.Copy, [ps], [st_])
                S.dma("sp", O["o_pmem"][i][mb * 128:(mb + 1) * 128, q * 512:(q + 1) * 512], st_[:, 0:512], reads=[st_],
                      key=st_)
                if DBG2 < 3:
                    continue
                dstb = KB if q < 2 else VB
                cp("dve", dstb[:, mb, (q % 2) * 512:(q % 2) * 512 + 512], ps[:, :], [ps], [dstb])
        if DBG < 2:
            return
        make_kt(KT, KB)
        if DBG < 3:
            return
        for ti in range(4):
            if DBG < 4 and ti > 0:
                return
            U = Unext if (ti > 0 and HOIST) else prenorm(ti, i * 6 + 2)
            for q in range(2):
                sl = take(("xq", i, ti, q))
                sv = sl[:, :].rearrange("p (k n) -> p k n", k=8)
                for m4 in range(4):
                    bg_step()
                    hc = q * 4 + m4
                    for st in TILES[ti]:
                        c0, w, lc = st
                        ps = PSM.get()
                        for kc in range(8):
                            mm(ps[:, :w], sv[:, kc, m4 * 128:(m4 + 1) * 128], U[:, kc, lc:lc + w], kc == 0, kc == 7,
                               [sl, U], ps)
                        act(qv[:, hc, lc:lc + w], ps[:, :w], AF.Copy, [ps], [B24[0]], scale=1.0 / 16.0)
            for st in TILES[ti]:
                c0, w, lc = st
                if w > NS:
                    pts = [xattn_a(PTR, KT, 0, w, lc)]
                    for h in range(4):
                        if h < 3:
                            pts.append(xattn_a(PTR, KT, h + 1, w, lc))
                        xattn_b(pts[h], VB, h, w, lc)
                else:
                    for n in range(NS):
                        cm = I["cmem"][i][n]
                        S.dma("pool", KB[:, :, :], cm[:, 0:1024].rearrange("(mb p) f -> p mb f", p=128), writes=[KB])
                        S.dma("pool", SVB[:, :, :], cm[:, 1024:2048].rearrange("(mb p) f -> p mb f", p=128), writes=[SVB])
                        make_kt(SKT, KB)
                        for h in range(4):
                            xattn_core(PTR, SKT, SVB, h, 1, lc + n)
            Unext = prenorm(ti + 1, i * 6 + 2) if (ti < 3 and HOIST) else None
            SQ = post_begin(ti)
            for q in range(2):
                sl = take(("xo", i, ti, q))
                sv = sl[:, :].rearrange("p (k n) -> p k n", k=8)
                for m4 in range(4):
                    mc = q * 4 + m4
                    for st in TILES[ti]:
                        c0, w, lc = st
                        pf = PSM.get()
                        for kc in range(8):
                            mm(pf[:, :w], sv[:, kc, m4 * 128:(m4 + 1) * 128], ov[:, kc, lc:lc + w], kc == 0, kc == 7,
                               [sl, B24[1]], pf)
                        post_evac(pf, mc, st, SQ)
            post_finish(ti, i * 6 + 3, SQ)

    def ssd(i):
        j = i // 2
        areset()
        HT32 = aalloc("ht32", [128, 2048], F32)
        HTB = aalloc("htb", [128, 2048], BF16)
        DF = aalloc("df", [128, 16], F32)
        XDT = aalloc("xdt", [128, 2048], BF16)
        YTM = aalloc("ytm", [128, 2048], BF16)
        BTM = aalloc("btm", [128, 512], BF16)
        CBM = [aalloc("cbm%d" % k, [128, 128], F32) for k in range(2)]
        XH = aalloc("xh", [128, 24, 3], F32)
        XHB = aalloc("xhb", [128, 24, 3], BF16)
        RAWB = Ring([aalloc("rawb%d" % k, [128, TW + 4], BF16) for k in range(3)])
        DGR = Ring([aalloc("dg%d" % k, [128, 4, 128], BF16) for k in range(2)])
        SXH = aalloc("sxh", [128, 24, 12], F32)
        XNEW = aalloc("xnew", [128, 24, NS], F32)
        WDT = aalloc("wdt", [128, 8, 32], BF16)
        PRM = aalloc("prm", [128, 3, 32], F32)
        DTA = aalloc("dta", [128, 5, 64], F32)
        EE = aalloc("ee", [128, 128], F32)
        xact = hv
        w_in = I["ssm_w_in"][j]
        S.dma("pool", WDT[:, :, :], w_in[:, DI + CONV:INW].rearrange("(k p) n -> p k n", p=128), writes=[WDT])
        S.dma("sp", PRM[:, 0, :], I["ssm_dt_bias"][j:j + 1, :].partition_broadcast(128), writes=[PRM])
        S.dma("sp", PRM[:, 1, :], I["ssm_a_log"][j:j + 1, :].partition_broadcast(128), writes=[PRM])
        S.dma("sp", PRM[:, 2, :], I["ssm_d"][j:j + 1, :].partition_broadcast(128), writes=[PRM])
        S.dma("sp", DF[0:64, :], I["ssm_d"][j:j + 1, 0:32:2].partition_broadcast(64), writes=[DF], allow_slow_non_contiguous=True)
        S.dma("sp", DF[64:128, :], I["ssm_d"][j:j + 1, 1:32:2].partition_broadcast(64), writes=[DF], allow_slow_non_contiguous=True)
        act(PRM[:, 1, :], PRM[:, 1, :], AF.Exp, [PRM], [PRM])
        tsc("dve", PRM[:, 1, :], PRM[:, 1, :], -1.0, None, ALU.mult, None, [PRM], [PRM])
        mset("pool", XH[:], 0.0, [XH])
        mset("pool", XHB[:], 0.0, [XHB])
        mset("pool", HT32[:], 0.0, [HT32])
        mset("pool", HTB[:], 0.0, [HTB])
        for b in range(6):
            st_ = STG.get()
            S.dma("sp", st_[0:12, 0:512], I["st_conv"][j].rearrange("n r f -> (n r) f")[:, b * 512:(b + 1) * 512], writes=[st_])
            ps = PST.get()
            for q in range(4):
                trp(ps[:, q * 12:(q + 1) * 12], st_[0:12, q * 128:(q + 1) * 128], identf[0:12, 0:12], [st_, identf], ps)
            cp("dve", SXH[:, b * 4:b * 4 + 4, :], ps[:, 0:48].rearrange("p (g r) -> p g r", r=12), [ps], [SXH])

        def state_out(dst):
            for q4 in range(4):
                st_ = STG.get()
                ps = PST.get()
                for q in range(4):
                    c_ = q4 * 4 + q
                    trp(ps[:, q * 128:(q + 1) * 128], HT32[:, c_ * 128:(c_ + 1) * 128], identf[:], [HT32, identf], ps)
                cp("act", st_[:, 0:512], ps[:, :], [ps], [st_])
                S.dma("sp", dst[q4 * 512:(q4 + 1) * 512, :].rearrange("(c p) k -> p c k", p=128),
                      st_[:, 0:512].rearrange("p (c k) -> p c k", c=4), reads=[st_], key=st_)

        for ti in range(4):
            U = Unext if (ti > 0 and HOIST) else prenorm(ti, i * 6 + 0)
            later = []
            for s_ in range(6):
                sl = take(("xbc", i, ti, s_))
                sv = sl[:, :].rearrange("p (k n) -> p k n", k=8)
                for c4 in range(4):
                    bg_step()
                    cc = s_ * 4 + c4
                    w0 = SCW[:, cc, j * 4 + 0:j * 4 + 1]
                    w1 = SCW[:, cc, j * 4 + 1:j * 4 + 2]
                    w2 = SCW[:, cc, j * 4 + 2:j * 4 + 3]
                    w3 = SCW[:, cc, j * 4 + 3:j * 4 + 4]
                    bb = SCW[:, cc, 8 + j:9 + j]
                    for st in TILES[ti]:
                        c0, w, lc = st
                        ps = PSM.get()
                        for kc in range(8):
                            mm(ps[:, :w], sv[:, kc, c4 * 128:(c4 + 1) * 128], U[:, kc, lc:lc + w], kc == 0, kc == 7, [sl, U], ps)
                        if w > NS:
                            dg = DGR.get()
                            tt("dve", dg[:, :, :], identb[:, :].unsqueeze(1).to_broadcast([128, 4, 128]),
                               SCW[:, cc, j * 4:j * 4 + 4].unsqueeze(2).to_broadcast([128, 4, 128]), ALU.mult, [identb, SCW], [dg])
                            raw = RAWB.get()
                            cp("pool", raw[:, 0:3], XHB[:, cc, :], [XHB], [raw])
                            act(raw[:, 3:3 + w], ps[:, :w], AF.Copy, [ps], [raw])
                            cp("pool", XHB[:, cc, :], raw[:, w:w + 3], [raw], [XHB])
                            if ti == 3:
                                act(XH[:, cc, :], ps[:, w - 3:w], AF.Copy, [ps], [XH])

                            def fin(raw=raw, cc=cc, w=w, lc=lc, bb=bb, dg=dg):
                                pc2 = PSM.get()
                                for k in range(4):
                                    mm(pc2[:, :w], dg[:, k, :], raw[:, k:k + w], k == 0, k == 3, [dg, raw], pc2)
                                ac = TMP.get()
                                act(ac[:, :w], pc2[:, :w], AF.Identity, [pc2, SCW], [ac], bias=bb)
                                th = TMP.get()
                                act(th[:, :w], ac[:, :w], AF.Tanh, [ac], [th])
                                stt(xact[:, cc, lc:lc + w], th[:, :w], 1.0, ac[:, :w], ALU.add, ALU.mult, [th, ac], [hbuf(cc)])
                            while later:
                                later.pop(0)()
                            later.append(fin)
                        else:
                            ac = TMP.get()
                            act(ac[:, :w], SXH[:, cc, 0:12:3], AF.Identity, [SXH, SCW], [ac], scale=w0, bias=bb)
                            stt(ac[:, :w], SXH[:, cc, 1:12:3], w1, ac[:, :w], ALU.mult, ALU.add, [SXH, SCW, ac], [ac])
                            stt(ac[:, :w], SXH[:, cc, 2:12:3], w2, ac[:, :w], ALU.mult, ALU.add, [SXH, SCW, ac], [ac])
                            stt(ac[:, :w], ps[:, :w], w3, ac[:, :w], ALU.mult, ALU.add, [ps, SCW, ac], [ac])
                            act(XNEW[:, cc, :], ps[:, :w], AF.Copy, [ps], [XNEW])
                            th = TMP.get()
                            act(th[:, :w], ac[:, :w], AF.Tanh, [ac], [th])
                            stt(xact[:, cc, lc:lc + w], th[:, :w], 1.0, ac[:, :w], ALU.add, ALU.mult, [th, ac], [hbuf(cc)])
            while later:
                later.pop(0)()
            if ti == 3:
                rows_out(lambda cc: XH[:, cc, :], [XH], 3, 24, O["o_pconv"][j])
                rows_out(lambda cc: XNEW[:, cc, :], [XNEW], NS, 24, O["o_sconv"][j][:, 2, :])
            blocks = [(c * 128, 128) for c in range(4)] + ([(512, NS)] if ti == 3 else [])
            for bi, (lcb, L) in enumerate(blocks):
                pd = PSX.get()
                for kc in range(8):
                    mm(pd[0:L, 0:32], U[:, kc, lcb:lcb + L], WDT[:, kc, :], kc == 0, kc == 7, [U, WDT], pd)
                tt("dve", DTA[0:L, bi, 0:32], pd[0:L, 0:32], PRM[0:L, 0, :], ALU.add, [pd, PRM], [DTA])
                act(DTA[0:L, bi, 0:32], DTA[0:L, bi, 0:32], AF.Exp, [DTA], [DTA])
                act(DTA[0:L, bi, 0:32], DTA[0:L, bi, 0:32], AF.Ln, [DTA], [DTA], bias=1.0)
                tt("dve", DTA[0:L, bi, 32:64], DTA[0:L, bi, 0:32], PRM[0:L, 1, :], ALU.mult, [DTA, PRM], [DTA])
            for bi, (lcb, L) in enumerate(blocks):
                dtv = DTA[0:L, bi, 0:32]
                av = DTA[0:L, bi, 32:64]
                for hf in range(2):
                    ps = PST.get()
                    pb = ps[:, :].bitcast(BF16)
                    for q in range(8):
                        cc = hf * 8 + q
                        trp(pb[0:L, q * 128:(q + 1) * 128], xact[:, cc, lcb:lcb + L], identb[:], [hbuf(cc), identb], ps)
                    tt("dve", XDT[0:L, hf * 1024:(hf + 1) * 1024].rearrange("p (h e) -> p h e", e=64),
                       pb[0:L, :].rearrange("p (h e) -> p h e", e=64),
                       dtv[:, hf * 16:(hf + 1) * 16].unsqueeze(2).to_broadcast([L, 16, 64]), ALU.mult, [ps, DTA], [XDT])
                ps = PST.get()
                pb = ps[:, :].bitcast(BF16)
                for q in range(4):
                    trp(pb[0:L, q * 128:(q + 1) * 128], xact[:, 16 + q, lcb:lcb + L], identb[:], [B24[2], identb], ps)
                cp("act", BTM[0:L, :], pb[0:L, 0:512], [ps], [BTM])
                if L == 128:
                    pa = PSX.get()
                    mm(pa[:, 0:32], trile_f[:], av, True, True, [trile_f, DTA], pa)
                    mm(pa[:, 32:64], onesf[:], av, True, True, [onesf, DTA], pa)
                    act(EE[:, 0:32], pa[:, 0:32], AF.Copy, [pa], [EE])
                    act(EE[:, 32:64], pa[:, 0:32], AF.Exp, [pa], [EE])
                    tt("dve", EE[:, 64:96], pa[:, 32:64], EE[:, 0:32], ALU.subtract, [pa, EE], [EE])
                    act(EE[:, 64:96], EE[:, 64:96], AF.Exp, [EE], [EE])
                    act(EE[:, 96:128], pa[:, 32:64], AF.Exp, [pa], [EE])
                    LTs, PSEGs, WTs, PYs = {}, {}, {}, {}

                    def sP(g):
                        pc = PSX.get()
                        mm(pc[:, 0:128], xact[:, 16 + g, lcb:lcb + 128], xact[:, 20 + g, lcb:lcb + 128], True, True, [B24[2]], pc)
                        tt("dve", CBM[g % 2][:, :], pc[:, 0:128], trile_f[:], ALU.mult, [pc, trile_f], [CBM[g % 2]])

                    def s1(n):
                        g, hf = divmod(n, 2)
                        h0 = g * 8 + hf * 4
                        Lt = TMP.get()
                        tt("pool", Lt[:, 0:512].rearrange("p (h s) -> p h s", h=4),
                           maskgt_f[:, :].unsqueeze(1).to_broadcast([128, 4, 128]),
                           av[:, h0:h0 + 4].unsqueeze(2).to_broadcast([128, 4, 128]), ALU.mult, [maskgt_f, DTA], [Lt])
                        LTs[n] = Lt

                    def s2(n):
                        g, hf = divmod(n, 2)
                        if hf == 0:
                            sP(g)
                        Lt = LTs[n]
                        pseg = PSM.get()
                        for h4 in range(4):
                            mm(pseg[:, h4 * 128:(h4 + 1) * 128], Lt[:, h4 * 128:(h4 + 1) * 128], trile_f[:], True, True,
                               [Lt, trile_f], pseg)
                        PSEGs[n] = pseg

                    def s3(n):
                        g, hf = divmod(n, 2)
                        dec = TMP.get()
                        act(dec[:, 0:512], PSEGs[n][:, :], AF.Exp, [PSEGs[n]], [dec])
                        WT = BT.get()
                        tt("dve", WT[:, 0:512].rearrange("p (h t) -> p h t", h=4),
                           dec[:, 0:512].rearrange("p (h t) -> p h t", h=4),
                           CBM[g % 2][:, :].unsqueeze(1).to_broadcast([128, 4, 128]), ALU.mult, [dec, CBM[g % 2]], [WT])
                        WTs[n] = WT

                    def s4(n):
                        g, hf = divmod(n, 2)
                        if hf == 0:
                            PYs[g] = PST.get()
                        py = PYs[g]
                        WT = WTs[n]
                        for h4 in range(4):
                            h = g * 8 + hf * 4 + h4
                            mm(py[:, (hf * 4 + h4) * 64:(hf * 4 + h4 + 1) * 64], WT[:, h4 * 128:(h4 + 1) * 128],
                               XDT[:, h * 64:(h + 1) * 64], True, True, [WT, XDT], py)

                    def sE(g):
                        gc = slice(g * 512, (g + 1) * 512)
                        py = PYs[g]
                        pyi = PSX.get()
                        mm(pyi[:, :], xact[:, 20 + g, lcb:lcb + 128], HTB[:, gc], True, True, [B24[2], HTB], pyi)
                        yt = TMP.get()
                        tt("dve", yt[:, 0:512].rearrange("p (h e) -> p h e", e=64), pyi[:, :].rearrange("p (h e) -> p h e", e=64),
                           EE[:, 32 + g * 8:32 + g * 8 + 8].unsqueeze(2).to_broadcast([128, 8, 64]), ALU.mult, [pyi, EE], [yt])
                        tt("dve", YTM[:, gc], yt[:, 0:512], py[:, :], ALU.add, [yt, py], [YTM])
                        xd = BT.get()
                        tt("pool", xd[:, 0:512].rearrange("p (h e) -> p h e", e=64), XDT[:, gc].rearrange("p (h e) -> p h e", e=64),
                           EE[:, 64 + g * 8:64 + g * 8 + 8].unsqueeze(2).to_broadcast([128, 8, 64]), ALU.mult, [XDT, EE], [xd])
                        pn = PSX.get()
                        mm(pn[:, :], BTM[:, g * 128:(g + 1) * 128], xd[:, 0:512], True, True, [BTM, xd], pn)
                        tt("pool", HT32[:, gc].rearrange("p (h e) -> p h e", e=64), HT32[:, gc].rearrange("p (h e) -> p h e", e=64),
                           EE[:, 96 + g * 8:96 + g * 8 + 8].unsqueeze(2).to_broadcast([128, 8, 64]), ALU.mult, [HT32, EE], [HT32])
                        tt("dve", HT32[:, gc], HT32[:, gc], pn[:, :], ALU.add, [HT32, pn], [HT32])
                        cp("act", HTB[:, gc], HT32[:, gc], [HT32], [HTB])

                    s1(0)
                    s1(1)
                    s2(0)
                    for n in range(8):
                        if n + 2 < 8:
                            s1(n + 2)
                        if n + 1 < 8:
                            s2(n + 1)
                        s3(n)
                        s4(n)
                        if n % 2 == 1:
                            sE(n // 2)
                    if ti == 3 and bi == 3:
                        state_out(O["o_pssm"][j])
                else:
                    mset("pool", YTM[0:NS, :], 0.0, [YTM])
                    for n in range(NS):
                        oh = identf[0:NS, n:n + 1]
                        for q4 in range(4):
                            st_ = STG.get()
                            S.dma("sp", st_[:, 0:512].rearrange("p (c k) -> p c k", c=4),
                                  I["st_ssm"][j][n][q4 * 512:(q4 + 1) * 512, :].rearrange("(c p) k -> p c k", p=128), writes=[st_])
                            ps = PST.get()
                            for q in range(4):
                                trp(ps[:, q * 128:(q + 1) * 128], st_[:, q * 128:(q + 1) * 128], identf[:], [st_, identf], ps)
                            cp("act", HT32[:, q4 * 512:(q4 + 1) * 512], ps[:, :], [ps], [HT32])
                        pa = PSX.get()
                        mm(pa[:, 0:32], identf[0:NS, n:n + 1].to_broadcast([NS, 128]), av, True, True, [identf, DTA], pa)
                        act(EE[:, 0:32], pa[:, 0:32], AF.Exp, [pa], [EE])
                        tt("dve", HT32[:, :].rearrange("p (h e) -> p h e", e=64), HT32[:, :].rearrange("p (h e) -> p h e", e=64),
                           EE[:, 0:32].unsqueeze(2).to_broadcast([128, 32, 64]), ALU.mult, [HT32, EE], [HT32])
                        for g in range(4):
                            gc = slice(g * 512, (g + 1) * 512)
                            xd = BT.get()
                            tsc("dve", xd[0:NS, 0:512], XDT[0:NS, gc], oh, None, ALU.mult, None, [XDT, identf], [xd])
                            pn = PSM.get()
                            mm(pn[:, :], BTM[0:NS, g * 128:(g + 1) * 128], xd[0:NS, 0:512], True, True, [BTM, xd], pn)
                            tt("dve", HT32[:, gc], HT32[:, gc], pn[:, :], ALU.add, [HT32, pn], [HT32])
                            cp("act", HTB[:, gc], HT32[:, gc], [HT32], [HTB])
                            py = PSM.get()
                            mm(py[0:NS, :], xact[:, 20 + g, lcb:lcb + NS], HTB[:, gc], True, True, [B24[2], HTB], py)
                            stt(YTM[0:NS, gc], py[0:NS, :], oh, YTM[0:NS, gc], ALU.mult, ALU.add, [py, identf, YTM], [YTM])
                        state_out(O["o_sssm"][j][n])
                for hf in range(2):
                    ps = PST.get()
                    pb = ps[:, :].bitcast(BF16)
                    for q in range(8):
                        fc = hf * 8 + q
                        trp(pb[:, q * L:(q + 1) * L], YTM[0:L, fc * 128:(fc + 1) * 128], identb[0:L, 0:L], [YTM, identb], ps)
                    for q in range(8):
                        fc = hf * 8 + q
                        stt(hv[:, fc, lcb:lcb + L], hv[:, fc, lcb:lcb + L], DF[:, fc:fc + 1], pb[:, q * L:(q + 1) * L], ALU.mult, ALU.add,
                            [B24[hf], DF, ps], [B24[hf]])
            zp = []
            zpend = []

            def zflush():
                while zp:
                    pb_, sq_, w_, c4_ = zp.pop(0)
                    mm(pb_[:, :w_], onesb[:], sq_[:, :w_], c4_ == 0, c4_ == 3, [onesb, sq_], pb_)

            for g in range(4):
                sl = take(("z", i, ti, g))
                sv = sl[:, :].rearrange("p (k n) -> p k n", k=8)
                pss = {st: PSX.get() for st in TILES[ti]}
                for c4 in range(4):
                    fc = g * 4 + c4
                    for st in TILES[ti]:
                        c0, w, lc = st
                        pz = PSM.get()
                        for kc in range(8):
                            mm(pz[:, :w], sv[:, kc, c4 * 128:(c4 + 1) * 128], U[:, kc, lc:lc + w], kc == 0, kc == 7, [sl, U], pz)
                        th = TMP.get()
                        act(th[:, :w], pz[:, :w], AF.Tanh, [pz], [th], scale=0.5)
                        stt(th[:, :w], th[:, :w], 1.0, pz[:, :w], ALU.add, ALU.mult, [th, pz], [th])
                        stt(hv[:, fc, lc:lc + w], th[:, :w], 0.5, hv[:, fc, lc:lc + w], ALU.mult, ALU.mult, [th, hbuf(fc)], [hbuf(fc)])
                        sq = BT.get()
                        act(sq[:, :w], hv[:, fc, lc:lc + w], AF.Square, [hbuf(fc)], [sq])
                        zflush()
                        zp.append((pss[st], sq, w, c4))
                zflush()

                def zfin(g=g, pss=pss):
                    for st in TILES[ti]:
                        c0, w, lc = st
                        v = RS.get()
                        act(v[:, :w], pss[st][:, :w], AF.Ln, [pss[st]], [v], scale=1.0 / 512.0, bias=EPS)
                        r = RS.get()
                        act(r[:, :w], v[:, :w], AF.Exp, [v], [r], scale=-0.5)
                        for c4 in range(4):
                            fc = g * 4 + c4
                            stt(hv[:, fc, lc:lc + w], hv[:, fc, lc:lc + w], SNW[:, fc, j:j + 1], r[:, :w], ALU.mult, ALU.mult,
                                [hbuf(fc), SNW, r], [hbuf(fc)])
                if len(TILES[ti]) == 1:
                    while zpend:
                        zpend.pop(0)()
                    zpend.append(zfin)
                else:
                    zfin()
            while zpend:
                zpend.pop(0)()
            Unext = prenorm(ti + 1, i * 6 + 0) if (ti < 3 and HOIST) else None
            SQ = post_begin(ti)
            for s_ in range(4):
                sl = take(("wout", i, ti, s_))
                sv = sl[:, :].rearrange("p (k n) -> p k n", k=16)
                for m2 in range(2):
                    mc = s_ * 2 + m2
                    for st in TILES[ti]:
                        c0, w, lc = st
                        pf = PSM.get()
                        for kc in range(16):
                            mm(pf[:, :w], sv[:, kc, m2 * 128:(m2 + 1) * 128], hv[:, kc, lc:lc + w], kc == 0, kc == 15,
                               [sl, hbuf(kc)], pf)
                        post_evac(pf, mc, st, SQ)
            post_finish(ti, i * 6 + 1, SQ)

    def att(i):
        j = i // 2
        areset()
        QT = aalloc("qt", [128, 6, TW], BF16)
        OB = aalloc("ob", [128, 12, TW], BF16)
        DENT = aalloc("dent", [128, 4, TW], F32)
        PTA = Ring([aalloc("pta%d" % k, [128, 512], BF16) for k in range(4)])
        VA = Ring([aalloc("va%d" % k, [128, 4, 64], BF16) for k in range(11)])
        KTA = b24[:, 0:6 * T].rearrange("p (k n) -> p k n", k=6)
        BALL = [B24[0], B24[1], B24[2]]
        caches = (("128", 128, 1), ("512", 512, 4), ("2048", 2048, 16))

        def kv_out(tm, L, colbase, slab, sample):
            kvoff = 0 if slab < 4 else 256
            lo = slab % 2
            pieces = []
            if lo == 0:
                pieces += [(0, 0, 256, 0), (1, 256, 128, 0)]
            else:
                pieces += [(1, 0, 128, 128), (2, 128, 256, 0)]
            for g, tc, ncol, oc in pieces:
                nm, W, dil = caches[g]
                if sample:
                    S.dma("sp", O["o_s" + nm][j][:, W - 1, kvoff + oc:kvoff + oc + ncol], tm[0:L, tc:tc + ncol], reads=[tm], key=tm)
                else:
                    r0 = colbase - (TP - W)
                    if r0 >= 0:
                        S.dma("sp", O["o_p" + nm][j][r0:r0 + L, kvoff + oc:kvoff + oc + ncol], tm[0:L, tc:tc + ncol], reads=[tm], key=tm)

        def unit_a(g, qsl, nq, kbs):
            PTs = []
            for kt_fn, nk, va, mask, rk in kbs:
                ps = PSM.get()
                for hh in (0, 2, 1, 3):
                    H = g * 4 + hh
                    pb_ = 64 * (H % 2)
                    mm(ps[0:nk, hh * nq:(hh + 1) * nq], kt_fn(hh), QT[pb_:pb_ + 64, H // 2, qsl], True, True, rk + [QT], ps, rg=pb_)
                PT = PTA.get()
                act(PT[0:nk, 0:4 * nq], ps[0:nk, 0:4 * nq], AF.Exp, [ps], [PT])
                if mask is not None:
                    tt("dve", PT[0:nk, 0:4 * nq].rearrange("p (h q) -> p h q", h=4), PT[0:nk, 0:4 * nq].rearrange("p (h q) -> p h q", h=4),
                       mask.unsqueeze(1).to_broadcast([nk, 4, nq]), ALU.mult, [PT, trile_b, trige_b, identb], [PT])
                PTs.append(PT)
            return PTs

        def unit_b(g, nq, kbs, PTs, osl):
            po = PSM.get()
            for hh in range(4):
                for bi, (kt_fn, nk, va, mask, rk) in enumerate(kbs):
                    mm(po[0:64, hh * nq:(hh + 1) * nq], va[0:nk, hh, :], PTs[bi][0:nk, hh * nq:(hh + 1) * nq], bi == 0,
                       bi == len(kbs) - 1, [va, PTs[bi]], po)
            pdn = PSX.get()
            for bi, (kt_fn, nk, va, mask, rk) in enumerate(kbs):
                mm(pdn[0:64, 0:4 * nq], onesb[0:nk, 0:64], PTs[bi][0:nk, 0:4 * nq], bi == 0, bi == len(kbs) - 1, [onesb, PTs[bi]], pdn)
            cp("act", OB[0:64, g * 4:g * 4 + 4, osl], po[0:64, 0:4 * nq].rearrange("p (h q) -> p h q", h=4), [po], [OB])
            if g == 0:
                cp("dve", DENT[0:64, :, osl], pdn[0:64, 0:4 * nq].rearrange("p (h q) -> p h q", h=4), [pdn], [DENT])
            else:
                tt("dve", DENT[0:64, :, osl], DENT[0:64, :, osl], pdn[0:64, 0:4 * nq].rearrange("p (h q) -> p h q", h=4), ALU.add,
                   [DENT, pdn], [DENT])

        def run_units(units):
            if not units:
                return
            cur = units[0]()
            cur_pt = unit_a(cur[0], cur[1], cur[2], cur[3])
            for n in range(len(units)):
                nxt = nxt_pt = None
                if n + 1 < len(units):
                    nxt = units[n + 1]()
                    nxt_pt = unit_a(nxt[0], nxt[1], nxt[2], nxt[3])
                unit_b(cur[0], cur[2], cur[3], cur_pt, cur[4])
                cur, cur_pt = nxt, nxt_pt

        def kt_all(g, cols):
            def f(hh):
                H = g * 4 + hh
                pb_ = 64 * (H % 2)
                return KTA[pb_:pb_ + 64, H // 2, cols]
            return f

        def load_va(rows, g, nk):
            va = VA.get()
            S.dma("pool", va[0:nk, :, :], vscr[rows, g * 256:(g + 1) * 256].rearrange("p (h e) -> p h e", e=64), reads=[VSCR], writes=[va])
            return va

        for ti in range(4):
            U = Unext if (ti > 0 and HOIST) else prenorm(ti, i * 6 + 0)
            t0 = ti * 512
            blocks = [(c * 128, 128, t0 + c * 128, ti * 4 + c) for c in range(4)] + ([(512, NS, TP, 16)] if ti == 3 else [])
            later = []
            for s_ in range(6):
                sl = take(("qkv", i, ti, s_))
                sv = sl[:, 0:8 * 384].rearrange("p (k n) -> p k n", k=8)
                for (lcb, L, colbase, blk) in blocks:
                    bg_step()
                    ps = PSM.get()
                    for kc in range(8):
                        mm(ps[0:L, 0:384], U[:, kc, lcb:lcb + L], sv[:, kc, :], kc == 0, kc == 7, [U, sl], ps)
                    while later:
                        later.pop(0)()
                    tm = TMP.get()
                    if s_ < 4:
                        act(tm[0:L, 0:384], ps[0:L, 0:384], AF.Copy, [ps], [tm], scale=(0.125 if s_ < 2 else 1.0))
                        v3 = tm[0:L, 0:384].rearrange("p (h e) -> p h e", e=64)
                        x1 = v3[:, :, 0:8]
                        x2 = v3[:, :, 8:16]
                        cs = ropet[0:L, blk, 0:8].unsqueeze(1).to_broadcast([L, 6, 8])
                        sn = ropet[0:L, blk, 8:16].unsqueeze(1).to_broadcast([L, 6, 8])
                        tms = [SM.get() for _ in range(4)]
                        tv = [t_[0:L, 0:48].rearrange("p (h e) -> p h e", e=8) for t_ in tms]
                        tt("dve", tv[0], x1, cs, ALU.mult, [tm, ropet], [tms[0]])
                        tt("dve", tv[1], x2, sn, ALU.mult, [tm, ropet], [tms[1]])
                        tt("dve", tv[2], x2, cs, ALU.mult, [tm, ropet], [tms[2]])
                        tt("dve", tv[3], x1, sn, ALU.mult, [tm, ropet], [tms[3]])
                        tt("dve", x1, tv[0], tv[1], ALU.subtract, [tms[0], tms[1]], [tm])
                        tt("dve", x2, tv[2], tv[3], ALU.add, [tms[2], tms[3]], [tm])
                        if s_ >= 2:
                            kv_out(tm, L, colbase, s_, L == NS)
                        tb = BT.get()
                        cp("act", tb[0:L, 0:384], tm[0:L, 0:384], [tm], [tb])
                        def fin(tb=tb, L=L, s_=s_, lcb=lcb, colbase=colbase):
                            pt_ = PST.get()
                            pb = pt_[:, :].bitcast(BF16)
                            for q in range(3):
                                trp(pb[:, q * L:(q + 1) * L], tb[0:L, q * 128:(q + 1) * 128], identb[0:L, 0:L], [tb, identb], pt_)
                            f0 = (s_ % 2) * 3
                            if s_ < 2:
                                cp("dve", QT[:, f0:f0 + 3, lcb:lcb + L], pb[:, 0:3 * L].rearrange("p (f t) -> p f t", f=3), [pt_], [QT])
                            else:
                                cp("dve", KTA[:, f0:f0 + 3, colbase:colbase + L], pb[:, 0:3 * L].rearrange("p (f t) -> p f t", f=3), [pt_], BALL)
                        later.append(fin)
                    else:
                        act(tm[0:L, 0:384], ps[0:L, 0:384], AF.Copy, [ps], [tm])
                        S.dma("sp", vscr[colbase:colbase + L, (s_ - 4) * 384:(s_ - 3) * 384], tm[0:L, 0:384], reads=[tm], writes=[VSCR], key=tm)
                        kv_out(tm, L, colbase, s_, L == NS)
            if ADBG < 2:
                return
            while later:
                later.pop(0)()
            units = []

            def mk_g0(c):
                def f():
                    tb_ = ti * 4 + c
                    kbs = []
                    if tb_ > 0:
                        rows = slice((tb_ - 1) * 128, tb_ * 128)
                        kbs.append((kt_all(0, rows), 128, load_va(rows, 0, 128), trige_b[:, :], BALL))
                    rows = slice(tb_ * 128, (tb_ + 1) * 128)
                    kbs.append((kt_all(0, rows), 128, load_va(rows, 0, 128), trile_b[:, :], BALL))
                    return (0, slice(c * 128, (c + 1) * 128), 128, kbs, slice(c * 128, (c + 1) * 128))
                return f

            def mk_g1(r):
                def f():
                    kbs = []
                    if ti > 0:
                        rows = slice((ti - 1) * 512 + r, ti * 512, 4)
                        kbs.append((kt_all(1, rows), 128, load_va(rows, 1, 128), trige_b[:, :], BALL))
                    rows = slice(ti * 512 + r, (ti + 1) * 512, 4)
                    kbs.append((kt_all(1, rows), 128, load_va(rows, 1, 128), trile_b[:, :], BALL))
                    return (1, slice(r, 512, 4), 128, kbs, slice(r, 512, 4))
                return f

            def mk_g2(r):
                def f():
                    nk = 32 * (ti + 1)
                    rows = slice(r, (ti + 1) * 512, 16)
                    kbs = [(kt_all(2, rows), nk, load_va(rows, 2, nk), trile_b[0:nk, 32 * ti:32 * ti + 32], BALL)]
                    return (2, slice(r, 512, 16), 32, kbs, slice(r, 512, 16))
                return f

            def mk_s(n, g):
                def f():
                    nm, W, dil = caches[g]
                    cach = I["c" + nm][j][n]
                    kc_ = VA.get()
                    S.dma("pool", kc_[:, :, :], cach[0:W:dil, 0:256].rearrange("p (h e) -> p h e", e=64), writes=[kc_])
                    vac = VA.get()
                    S.dma("pool", vac[:, :, :], cach[0:W:dil, 256:512].rearrange("p (h e) -> p h e", e=64), writes=[vac])
                    ktc = VA.get()
                    pt_ = PST.get()
                    pb = pt_[:, :].bitcast(BF16)
                    kcf = kc_[:, :, :].rearrange("p h e -> p (h e)")
                    for q in range(2):
                        trp(pb[:, q * 128:(q + 1) * 128], kcf[:, q * 128:(q + 1) * 128], identb[:], [kc_, identb], pt_)
                    ktf = ktc[:, :, :].rearrange("p h e -> p (h e)")
                    cp("dve", ktf[:, :], pb[:, 0:256], [pt_], [ktc])

                    def ktc_fn(hh, ktf=ktf):
                        pb_ = 64 * (hh % 2)
                        return ktf[pb_:pb_ + 64, (hh // 2) * 128:(hh // 2) * 128 + 128]
                    van = load_va(slice(TP, T), g, NS)
                    kbs = [(ktc_fn, 128, vac, None, [ktc]),
                           (kt_all(g, slice(TP, T)), NS, van, identb[0:NS, n:n + 1], BALL)]
                    return (g, slice(512 + n, 513 + n), 1, kbs, slice(512 + n, 513 + n))
                return f

            units += [mk_g0(c) for c in range(4)]
            units += [mk_g1(r) for r in range(4)]
            units += [mk_g2(r) for r in range(16)]
            if ti == 3:
                units += [mk_s(n, g) for g in range(3) for n in range(NS)]
            run_units(units)
            if ADBG < 7:
                continue
            for st in TILES[ti]:
                c0, w, lc = st
                act(DENT[0:64, :, lc:lc + w], DENT[0:64, :, lc:lc + w], AF.Ln, [DENT], [DENT])
                act(DENT[0:64, :, lc:lc + w], DENT[0:64, :, lc:lc + w], AF.Exp, [DENT], [DENT], scale=-1.0)
                for g in range(3):
                    tt("dve" if g != 1 else "pool", OB[0:64, g * 4:g * 4 + 4, lc:lc + w], OB[0:64, g * 4:g * 4 + 4, lc:lc + w],
                       DENT[0:64, :, lc:lc + w], ALU.mult, [OB, DENT], [OB])
            Unext = prenorm(ti + 1, i * 6 + 0) if (ti < 3 and HOIST) else None
            SQ = post_begin(ti)
            for s_ in range(4):
                sl = take(("wo", i, ti, s_))
                sv = sl[0:64, 0:12 * 256].rearrange("p (k n) -> p k n", k=12)
                for m2 in range(2):
                    mc = s_ * 2 + m2
                    for st in TILES[ti]:
                        c0, w, lc = st
                        pf = PSM.get()
                        for H in range(12):
                            mm(pf[:, :w], sv[:, H, m2 * 128:(m2 + 1) * 128], OB[0:64, H, lc:lc + w], H == 0, H == 11, [sl, OB], pf)
                        post_evac(pf, mc, st, SQ)
            post_finish(ti, i * 6 + 1, SQ)

    def mixer(i):
        if i % 2 == 0:
            ssd(i)
        else:
            att(i)

    for j in range(2):
        for nm, W in (("128", 128), ("512", 512), ("2048", 2048)):
            S.dma("sp", O["o_s" + nm][j][:, 0:W - 1, :], I["c" + nm][j][:, 1:W, :], key="outc")
        S.dma("sp", O["o_sconv"][j][:, 0:2, :], I["st_conv"][j][:, 1:3, :], key="outc")
    for kind, i in stages:
        if kind == "mix":
            mixer(i)
        elif kind == "xa":
            xattn(i)
        else:
            ffn(i)
    assert DBG < 99 or ADBG < 99 or wst["taken"] == len(plan), (wst["taken"], len(plan))

    bg_drain()
    for tb in range(16):
        for hh in range(2):
            st_ = STG.get()
            ps = PSM.get()
            for q in range(4):
                kc = hh * 4 + q
                trp(ps[:, q * 128:(q + 1) * 128], xT[:, kc, tb * 128:(tb + 1) * 128], identf[:], [XT[tb // 4], identf], ps)
            cp("act" if hh else "dve", st_[:, 0:512], ps[:, :], [ps], [st_])
            S.dma("sp", O["y_p"][tb * 128:(tb + 1) * 128, hh * 512:(hh + 1) * 512], st_[:, 0:512], reads=[st_], key=st_)
    for hh in range(2):
        st_ = STG.get()
        ps = PSM.get()
        for q in range(4):
            kc = hh * 4 + q
            trp(ps[0:NS, q * 128:(q + 1) * 128], xT[:, kc, TP:T], identf[:], [XT[3], identf], ps)
        cp("dve", st_[0:NS, 0:512], ps[0:NS, :], [ps], [st_])
        S.dma("sp", O["y_s"][:, hh * 512:(hh + 1) * 512], st_[0:NS, 0:512], reads=[st_], key=st_)

    S.finish()
    S.emit()
    es.close()
    return nc


def rope_table():
    half = 8
    inv = (np.float32(500000.0) ** (-(np.arange(half, dtype=np.float32) / np.float32(half)))).astype(np.float32)
    pos = np.concatenate([np.arange(TP, dtype=np.float32), np.full((NS,), 8192.0, np.float32)])
    ang = (pos[:, None] * inv[None, :]).astype(np.float32)
    return np.concatenate([np.cos(ang), np.sin(ang)], axis=1).astype(np.float32)


def make_in_maps(inp, cores):
    f = lambda a: np.ascontiguousarray(np.asarray(a, dtype=np.float32))
    shared = {
        "norms": f(inp["norms"]).reshape(24, D), "ssm_w_in": f(inp["ssm_w_in"]),
        "ssm_conv_w": f(inp["ssm_conv_w"]).reshape(8, CONV), "ssm_conv_b": f(inp["ssm_conv_b"]),
        "ssm_dt_bias": f(inp["ssm_dt_bias"]), "ssm_a_log": f(inp["ssm_a_log"]), "ssm_d": f(inp["ssm_d"]),
        "ssm_norm_w": f(inp["ssm_norm_w"]), "ssm_w_out": f(inp["ssm_w_out"]), "att_w_qkv": f(inp["att_w_qkv"]),
        "att_w_o": f(inp["att_w_o"]), "mem_norm": f(inp["mem_norm"]), "xa_w_q": f(inp["xa_w_q"]),
        "xa_w_kv": f(inp["xa_w_kv"]), "xa_w_o": f(inp["xa_w_o"]), "ffn_w_gu": f(inp["ffn_w_gu"]),
        "ffn_conv_w": f(inp["ffn_conv_w"]).reshape(12, DFF), "ffn_conv_b": f(inp["ffn_conv_b"]),
        "ffn_w_down": f(inp["ffn_w_down"]), "rope": rope_table(),
    }
    maps = []
    for c in cores:
        s = slice(NS * c, NS * c + NS)
        m = dict(shared)
        m["xp"] = f(inp["x_prompt"][c])
        m["xs"] = f(inp["x_sample"][s, 0])
        m["mem"] = f(inp["mem_prompt"][c])
        m["st_ssm"] = f(inp["state_ssm"][:, s]).reshape(2, NS, 2048, 128)
        m["st_conv"] = f(inp["state_ssm_conv"][:, s])
        m["c128"] = f(inp["cache_swa_kv_w128"][:, s]).reshape(2, NS, 128, 512)
        m["c512"] = f(inp["cache_swa_kv_w512"][:, s]).reshape(2, NS, 512, 512)
        m["c2048"] = f(inp["cache_swa_kv_w2048"][:, s]).reshape(2, NS, 2048, 512)
        m["cmem"] = f(inp["cache_mem_kv"][:, s]).reshape(4, NS, 256, 2048)
        m["st_ffn"] = f(inp["state_ffn_conv"][:, s])
        maps.append(m)
    return maps


def gather(results):
    R = results
    n = len(R)
    cat = lambda k, ax: np.concatenate([np.asarray(r[k]) for r in R], axis=ax)
    stk = lambda k: np.stack([np.asarray(r[k]) for r in R], axis=1)
    y_p = np.stack([np.asarray(r["y_p"]) for r in R], 0)
    y_s = cat("y_s", 0).reshape(n * NS, 1, D)
    return (
        y_p, y_s,
        stk("o_pssm").reshape(2, n, 32, 64, 128), stk("o_pconv"),
        stk("o_p128").reshape(2, n, 128, 2, 4, 64), stk("o_p512").reshape(2, n, 512, 2, 4, 64),
        stk("o_p2048").reshape(2, n, 2048, 2, 4, 64), stk("o_pmem").reshape(4, n, 256, 2, 4, 256), stk("o_pffn"),
        cat("o_sssm", 1).reshape(2, n * NS, 32, 64, 128), cat("o_sconv", 1),
        cat("o_s128", 1).reshape(2, n * NS, 128, 2, 4, 64), cat("o_s512", 1).reshape(2, n * NS, 512, 2, 4, 64),
        cat("o_s2048", 1).reshape(2, n * NS, 2048, 2, 4, 64), cat("o_sffn", 1),
    )


def kernel(**inputs):
    nc = build()
    maps = make_in_maps(inputs, list(range(NCORES)))
    res = run_bass_kernel_spmd(nc, maps, core_ids=list(range(NCORES)))
    return tuple(np.ascontiguousarray(a, dtype=np.float32) for a in gather(res.results))
```

```python
import contextlib
import numpy as np
import concourse.bass as bass
import concourse.mybir as mybir
from concourse.bass_utils import run_bass_kernel_spmd

F32 = mybir.dt.float32
BF16 = mybir.dt.bfloat16
AF = mybir.ActivationFunctionType
ALU = mybir.AluOpType

NCORES = 8
DMA_ROT = 16
EPS = 1e-6


class Buf:
    __slots__ = ("name", "t", "writers", "readers", "psum", "gdeps")

    def __init__(self, name, t=None, psum=False):
        self.name = name
        self.t = t
        self.writers = {}
        self.readers = {}
        self.gdeps = []
        self.psum = psum

    def __getitem__(self, k):
        return self.t[k]


class Op:
    __slots__ = ("eng", "fn", "waits", "signal", "seq", "key", "val", "snap", "dma")


class Sched:
    ENGS = ("pe", "act", "dve", "pool", "sp")

    def __init__(self, nc):
        self.nc = nc
        self.ops = {e: [] for e in self.ENGS}
        self.known = {e: {} for e in self.ENGS}
        self.snap = {e: None for e in self.ENGS}
        self.dma_count = {}
        self.rot_n = {}
        self.rot_last = {}
        self.n_ops = 0

    def _deps(self, reads, writes, nowaw, lane=None):
        deps = []
        for b in reads:
            for o in b.writers.values():
                deps.append((o, True))
            if b.psum:
                for ln, o in b.readers.items():
                    if ln != lane:
                        deps.append((o, False))
        for b in writes:
            for o in b.readers.values():
                deps.append((o, False))
            if b.readers or not nowaw:
                for o in b.writers.values():
                    deps.append((o, False))
            else:
                for o in b.gdeps:
                    deps.append((o, False))
        return deps

    def _register(self, op, lane, reads, writes, nowaw):
        for b in writes:
            if b.readers or not nowaw:
                b.gdeps = list(b.readers.values()) + list(b.writers.values())
                b.writers = {lane: op}
                b.readers = {}
            else:
                b.writers[lane] = op
        for b in reads:
            b.readers[lane] = op

    def _place(self, op, eng, deps):
        known = self.known[eng]
        waits = {}
        changed = False
        for (o, raw) in deps:
            if not o.dma and o.eng == eng and eng == "pe" and raw != "force":
                continue
            if known.get(o.key, 0) >= o.val:
                continue
            waits[o.key] = max(waits.get(o.key, 0), o.val)
            o.signal = True
            if o.snap is not None:
                for k, v in o.snap.items():
                    if known.get(k, 0) < v:
                        known[k] = v
            if known.get(o.key, 0) < o.val:
                known[o.key] = o.val
            changed = True
        if changed or self.snap[eng] is None:
            self.snap[eng] = dict(known)
        op.snap = self.snap[eng]
        op.waits = list(waits.items())
        self.ops[eng].append(op)
        self.n_ops += 1

    def add(self, eng, fn, reads=(), writes=(), nowaw=True, force=()):
        op = Op()
        op.eng = eng
        op.fn = fn
        op.signal = False
        op.dma = False
        op.seq = len(self.ops[eng]) + 1
        op.key = eng
        op.val = op.seq
        deps = self._deps(reads, writes, nowaw, eng)
        for o in force:
            deps.append((o, "force"))
        self._place(op, eng, deps)
        self._register(op, eng, reads, writes, nowaw)
        return op

    def dma(self, queue, out, in_, reads=(), writes=(), key=None, nowaw=True, **kw):
        op = Op()
        op.eng = queue
        op.dma = True
        op.signal = True
        op.seq = len(self.ops[queue]) + 1
        deps = self._deps(reads, writes, nowaw)
        if isinstance(key, str):
            kname = ("dma", key)
            cnt = self.dma_count.get(kname, 0) + 1
        else:
            n = self.rot_n.get(queue, 0)
            self.rot_n[queue] = n + 1
            kname = ("dma", queue, n % DMA_ROT)
            cnt = n // DMA_ROT + 1
            prev = self.rot_last.get(kname)
            if prev is not None:
                deps.append((prev, False))
            self.rot_last[kname] = op
        self.dma_count[kname] = cnt
        op.key = kname
        op.val = cnt
        op.fn = lambda e: e.dma_start(out=out, in_=in_, **kw)
        self._place(op, queue, deps)
        self._register(op, kname, reads, writes, nowaw)
        return op

    def finish(self):
        op = Op()
        op.eng = "sp"
        op.fn = None
        op.signal = False
        op.dma = False
        op.seq = len(self.ops["sp"]) + 1
        op.key = "sp"
        op.val = op.seq
        op.snap = None
        op.waits = [(k, c) for k, c in self.dma_count.items()]
        self.ops["sp"].append(op)

    def emit(self):
        nc = self.nc
        with contextlib.ExitStack() as es:
            sems = {}
            for e in ("pe", "act", "dve", "pool"):
                sems[e] = es.enter_context(nc.semaphore("s_" + e))
            for i, k in enumerate(self.dma_count):
                sems[k] = es.enter_context(nc.semaphore("d%d" % i))
            sigval = {}
            for e in ("pe", "act", "dve", "pool"):
                c = 0
                tab = {}
                for op in self.ops[e]:
                    if not op.dma and op.signal:
                        c += 1
                        tab[op.seq] = c
                sigval[e] = tab
            block = es.enter_context(nc.Block())

            def run(ename, eng):
                for op in self.ops[ename]:
                    for (k, v) in op.waits:
                        if isinstance(k, tuple):
                            eng.wait_ge(sems[k], 16 * v)
                        else:
                            eng.wait_ge(sems[k], sigval[k][v])
                    if op.fn is None:
                        continue
                    ins = op.fn(eng)
                    if op.dma:
                        ins.then_inc(sems[op.key], 16)
                    elif op.signal:
                        ins.then_inc(sems[ename], 1)

            @block.tensor
            def _(e):
                run("pe", e)

            @block.scalar
            def _(e):
                run("act", e)

            @block.vector
            def _(e):
                run("dve", e)

            @block.gpsimd
            def _(e):
                run("pool", e)

            @block.sync
            def _(e):
                run("sp", e)


class Ring:
    def __init__(self, bufs):
        self.bufs = bufs
        self.i = 0

    def get(self):
        b = self.bufs[self.i % len(self.bufs)]
        self.i += 1
        return b


D = 1024
TP = 2048
NS = 4
T = TP + NS
DI = 2048
NH = 32
NFF = 22
DFF = 2816
CONV = 3072
INW = 5152
DEPTH = 4
TW = 516
SLOT = 4096
NSLOT = 4
ARENA_BYTES = 36 * 1024
MIXERS_ON = True
import os
DBG = int(os.environ.get('KDBG', '99'))
DBG2 = int(os.environ.get('KDBG2', '99'))
ADBG = int(os.environ.get('ADBG', '99'))
HOIST = int(os.environ.get('HOIST', '1'))
BGON = int(os.environ.get('BGON', '0'))
DEFER = int(os.environ.get('DEFER', '1'))

IN_SPECS = [
    ("xp", [TP, D]), ("xs", [NS, D]), ("mem", [256, D]),
    ("st_ssm", [2, NS, 32 * 64, 128]), ("st_conv", [2, NS, 3, CONV]),
    ("c128", [2, NS, 128, 512]), ("c512", [2, NS, 512, 512]), ("c2048", [2, NS, 2048, 512]),
    ("cmem", [4, NS, 256, 2048]), ("st_ffn", [4, NS, 2, DFF]),
    ("norms", [24, D]), ("ssm_w_in", [2, D, INW]), ("ssm_conv_w", [8, CONV]), ("ssm_conv_b", [2, CONV]),
    ("ssm_dt_bias", [2, 32]), ("ssm_a_log", [2, 32]), ("ssm_d", [2, 32]), ("ssm_norm_w", [2, DI]),
    ("ssm_w_out", [2, DI, D]), ("att_w_qkv", [2, D, 2304]), ("att_w_o", [2, 768, D]), ("mem_norm", [4, D]),
    ("xa_w_q", [4, D, D]), ("xa_w_kv", [4, D, 2 * D]), ("xa_w_o", [4, D, D]), ("ffn_w_gu", [4, D, 2 * DFF]),
    ("ffn_conv_w", [12, DFF]), ("ffn_conv_b", [4, DFF]), ("ffn_w_down", [4, DFF, D]), ("rope", [T, 16]),
]
OUT_SPECS = [
    ("y_p", [TP, D]), ("y_s", [NS, D]), ("o_pssm", [2, 2048, 128]), ("o_pconv", [2, 3, CONV]),
    ("o_p128", [2, 128, 512]), ("o_p512", [2, 512, 512]), ("o_p2048", [2, 2048, 512]),
    ("o_pmem", [4, 256, 2048]), ("o_pffn", [4, 2, DFF]),
    ("o_sssm", [2, NS, 2048, 128]), ("o_sconv", [2, NS, 3, CONV]),
    ("o_s128", [2, NS, 128, 512]), ("o_s512", [2, NS, 512, 512]), ("o_s2048", [2, NS, 2048, 512]),
    ("o_sffn", [4, NS, 2, DFF]),
]
TILES = [[(0, 512, 0)], [(512, 512, 0)], [(1024, 512, 0)], [(1536, 512, 0), (2048, 4, 512)]]


def build(stop_after=None):
    nc = bass.Bass("TRN2", target_bir_lowering=False)
    S = Sched(nc)
    I = {n: nc.dram_tensor(n, s, F32, kind="ExternalInput").ap() for n, s in IN_SPECS}
    O = {n: nc.dram_tensor(n, s, F32, kind="ExternalOutput").ap() for n, s in OUT_SPECS}
    vscr = nc.dram_tensor("vscr", [T, 768], F32, kind="Internal").ap()
    VSCR = Buf("vscr")
    es = contextlib.ExitStack()
    cnt = [0]

    def sb(name, shape, dt):
        return Buf(name, es.enter_context(nc.sbuf_tensor(name, shape, dt)))

    def ring(name, n, shape, dt):
        return Ring([sb("%s%d" % (name, k), shape, dt) for k in range(n)])

    last_rg = {}

    def mm(out, lhsT, rhs, start, stop, reads, wbuf, rg=0):
        prev = last_rg.get(wbuf.name)
        force = [prev[0]] if (prev is not None and prev[1] != rg) else []
        op = S.add("pe", lambda e: e.matmul(out, lhsT=lhsT, rhs=rhs, start=start, stop=stop), reads, [wbuf], force=force)
        last_rg[wbuf.name] = (op, rg)

    def trp(out, in_, ident, reads, wbuf):
        S.add("pe", lambda e: e.transpose(out=out, in_=in_, identity=ident), reads, [wbuf])

    def act(out, in_, func, reads, writes, **kw):
        S.add("act", lambda e: e.activation(out=out, in_=in_, func=func, **kw), reads, writes)

    def tt(eng, out, in0, in1, op, reads, writes):
        S.add(eng, lambda e: e.tensor_tensor(out=out, in0=in0, in1=in1, op=op), reads, writes)

    def stt(out, in0, scalar, in1, op0, op1, reads, writes):
        S.add("dve", lambda e: e.scalar_tensor_tensor(out=out, in0=in0, scalar=scalar, in1=in1, op0=op0, op1=op1),
              reads, writes)

    def tsc(eng, out, in0, s1, s2, op0, op1, reads, writes):
        if s2 is None:
            S.add(eng, lambda e: e.tensor_scalar(out=out, in0=in0, scalar1=s1, scalar2=None, op0=op0), reads, writes)
        else:
            S.add(eng, lambda e: e.tensor_scalar(out=out, in0=in0, scalar1=s1, scalar2=s2, op0=op0, op1=op1),
                  reads, writes)

    def cp(eng, out, in_, reads, writes):
        if eng == "act":
            S.add("act", lambda e: e.activation(out=out, in_=in_, func=AF.Copy), reads, writes)
        else:
            S.add(eng, lambda e: e.tensor_copy(out=out, in_=in_), reads, writes)

    def recip(out, in_, reads, writes):
        S.add("dve", lambda e: e.reciprocal(out=out, in_=in_), reads, writes)

    def mset(eng, out, val, writes):
        S.add(eng, lambda e: e.memset(out, val), (), writes, nowaw=False)

    xT = es.enter_context(nc.sbuf_tensor("xT", [128, 8, T], F32))
    XT = [Buf("xT%d" % k, xT) for k in range(4)]
    slots = [sb("wslot%d" % k, [128, SLOT], BF16) for k in range(NSLOT)]
    URING = ring("U", 1, [128, 8, TW], BF16)
    b24 = es.enter_context(nc.sbuf_tensor("b24", [128, 24 * TW], BF16))
    B24 = [Buf("b24_%d" % k, b24) for k in range(3)]
    FB = sb("fsb", [128, 8, TW], BF16)
    PSM = Ring([Buf("psm%d" % k, es.enter_context(nc.psum_tensor("psm%d" % k, [128, 512], F32)), psum=True) for k in range(4)])
    PSX = Ring([Buf("psx%d" % k, es.enter_context(nc.psum_tensor("psx%d" % k, [128, 512], F32)), psum=True) for k in range(2)])
    PST = Ring([Buf("pst%d" % k, es.enter_context(nc.psum_tensor("pst%d" % k, [128, 512], F32)), psum=True) for k in range(2)])
    RS = ring("rs", 3, [128, TW + 4], F32)
    TMP = ring("tmp", 6, [128, TW + 4], F32)
    BT = ring("bt", 3, [128, TW + 4], BF16)
    SM = ring("sm", 8, [128, 64], F32)
    STG = TMP
    RPOST = ring("rpost", 2, [128, TW + 4], F32) if BGON else None

    arena = es.enter_context(nc.sbuf_tensor("arena", [128, ARENA_BYTES // 4], F32))
    ast = {"off": 0, "bufs": [], "fence": None}

    def aalloc(name, shape, dt):
        n = 1
        for d_ in shape[1:]:
            n *= d_
        nbytes = n * (4 if dt == F32 else 2)
        nbytes = (nbytes + 63) // 64 * 64
        off = ast["off"]
        assert off + nbytes <= ARENA_BYTES, (name, off, nbytes)
        ast["off"] = off + nbytes
        ap = arena[:, off // 4:(off + nbytes) // 4]
        if dt == BF16:
            ap = ap.bitcast(BF16)
        ap = ap[:, 0:n]
        if len(shape) == 3:
            ap = ap.rearrange("p (a b) -> p a b", a=shape[1])
        elif len(shape) == 4:
            ap = ap.rearrange("p (a b c) -> p a b c", a=shape[1], b=shape[2])
        b = Buf("A_" + name, ap)
        if ast["fence"] is not None:
            b.readers = {"pool": ast["fence"]}
            b.gdeps = [ast["fence"]]
        ast["bufs"].append(b)
        return b

    def areset():
        if ast["bufs"]:
            dm = SM.get()
            ast["fence"] = S.add("pool", lambda e: e.memset(dm[:, 0:1], 0.0), [], ast["bufs"] + [dm], nowaw=False)
        ast["off"] = 0
        ast["bufs"] = []

    identf = sb("identf", [128, 128], F32)
    identb = sb("identb", [128, 128], BF16)
    onesf = sb("onesf", [128, 128], F32)
    onesb = sb("onesb", [128, 128], BF16)
    trile_f = sb("trile_f", [128, 128], F32)
    trile_b = sb("trile_b", [128, 128], BF16)
    trige_b = sb("trige_b", [128, 128], BF16)
    maskgt_f = sb("maskgt_f", [128, 128], F32)
    neghalf = sb("neghalf", [128, 1], F32)
    GT = sb("gT", [128, 8, 24], F32)
    SCW = sb("scw", [128, 24, 10], F32)
    SNW = sb("snw", [128, 16, 2], F32)
    FCW = sb("fcw", [128, NFF, 16], F32)
    ropet = sb("ropet", [128, 17, 16], F32)

    mset("pool", onesf[:], 1.0, [onesf])
    mset("pool", neghalf[:], -0.5, [neghalf])
    S.add("pool", lambda e: e.affine_select(out=identf[:], in_=onesf[:], pattern=[[-1, 128]], compare_op=ALU.is_equal,
                                            fill=0.0, base=0, channel_multiplier=1), [onesf], [identf], nowaw=False)
    S.add("pool", lambda e: e.affine_select(out=trile_f[:], in_=onesf[:], pattern=[[1, 128]], compare_op=ALU.is_ge,
                                            fill=0.0, base=0, channel_multiplier=-1), [onesf], [trile_f], nowaw=False)
    S.add("pool", lambda e: e.affine_select(out=maskgt_f[:], in_=onesf[:], pattern=[[-1, 128]], compare_op=ALU.is_gt,
                                            fill=0.0, base=0, channel_multiplier=1), [onesf], [maskgt_f], nowaw=False)
    S.add("pool", lambda e: e.affine_select(out=trige_b[:], in_=onesf[:], pattern=[[-1, 128]], compare_op=ALU.is_ge,
                                            fill=0.0, base=0, channel_multiplier=1), [onesf], [trige_b], nowaw=False)
    cp("dve", identb[:], identf[:], [identf], [identb])
    cp("dve", onesb[:], onesf[:], [onesf], [onesb])
    cp("dve", trile_b[:], trile_f[:], [trile_f], [trile_b])
    S.dma("sp", ropet[:, 0:16, :], I["rope"][0:TP, :].rearrange("(c p) f -> p c f", p=128), writes=[ropet])
    S.dma("sp", ropet[0:NS, 16, :], I["rope"][TP:T, :], writes=[ropet])

    def load_featT(dst, srcs, F):
        R = sum(r for _, r in srcs)
        nj = F // 128
        for b in range((nj + 3) // 4):
            st_ = STG.get()
            j0 = b * 4
            jn = min(4, nj - j0)
            r0 = 0
            for ap, r in srcs:
                S.dma("sp", st_[r0:r0 + r, 0:jn * 128], ap[:, j0 * 128:(j0 + jn) * 128], writes=[st_])
                r0 += r
            ps = PST.get()
            for q in range(jn):
                trp(ps[:, q * R:(q + 1) * R], st_[0:R, q * 128:(q + 1) * 128], identf[0:R, 0:R], [st_, identf], ps)
            cp("dve", dst[:, j0:j0 + jn, :], ps[:, 0:jn * R].rearrange("p (g r) -> p g r", r=R), [ps], [dst])

    load_featT(GT, [(I["norms"], 24)], D)
    load_featT(SCW, [(I["ssm_conv_w"], 8), (I["ssm_conv_b"], 2)], CONV)
    load_featT(SNW, [(I["ssm_norm_w"], 2)], DI)
    load_featT(FCW, [(I["ffn_conv_w"], 12), (I["ffn_conv_b"], 4)], DFF)
    tsc("dve", SCW[:], SCW[:], 0.5, None, ALU.mult, None, [SCW], [SCW])
    tsc("dve", FCW[:], FCW[:], 0.5, None, ALU.mult, None, [FCW], [FCW])

    def rows_out(src_fn, rbufs, R, nj, dst):
        for b in range((nj + 3) // 4):
            j0 = b * 4
            jn = min(4, nj - j0)
            st_ = STG.get()
            ps = PST.get()
            for q in range(jn):
                trp(ps[0:R, q * 128:(q + 1) * 128], src_fn(j0 + q), identf[:], rbufs + [identf], ps)
            cp("dve", st_[0:R, 0:jn * 128], ps[0:R, 0:jn * 128], [ps], [st_])
            S.dma("sp", dst[:, j0 * 128:(j0 + jn) * 128], st_[0:R, 0:jn * 128], reads=[st_], key=st_)

    for tb in range(16):
        for hh in range(2):
            st_ = STG.get()
            S.dma("sp", st_[:, 0:512], I["xp"][tb * 128:(tb + 1) * 128, hh * 512:(hh + 1) * 512], writes=[st_])
            ps = PSM.get()
            for q in range(4):
                trp(ps[:, q * 128:(q + 1) * 128], st_[:, q * 128:(q + 1) * 128], identf[:], [st_, identf], ps)
            cp("act" if hh else "dve", xT[:, hh * 4:hh * 4 + 4, tb * 128:(tb + 1) * 128],
               ps[:, :].rearrange("p (k t) -> p k t", k=4), [ps], [XT[tb // 4]])
    for hh in range(2):
        st_ = STG.get()
        S.dma("sp", st_[0:NS, 0:512], I["xs"][:, hh * 512:(hh + 1) * 512], writes=[st_])
        ps = PST.get()
        for q in range(4):
            trp(ps[:, q * NS:(q + 1) * NS], st_[0:NS, q * 128:(q + 1) * 128], identf[0:NS, 0:NS], [st_, identf], ps)
        cp("dve", xT[:, hh * 4:hh * 4 + 4, TP:T], ps[:, 0:4 * NS].rearrange("p (k t) -> p k t", k=4), [ps], [XT[3]])

    plan = []

    def P_(tag, parts):
        plan.append((tag, parts))

    def kview(n_k, ncols, off=0):
        return lambda s: s[:, off:off + n_k * ncols].rearrange("p (k n) -> p k n", k=n_k)

    def wsrc(ap2d, r0, nk, c0, n):
        return ap2d[r0:r0 + nk * 128, c0:c0 + n].rearrange("(k p) n -> p k n", p=128)

    def plan_mix(i):
        j = i // 2
        for ti in range(4):
            if i % 2 == 0:
                w_in = I["ssm_w_in"][j]
                for s_ in range(6):
                    P_(("xbc", i, ti, s_), [(kview(8, 512), wsrc(w_in, 0, 8, DI + s_ * 512, 512))])
                for g in range(4):
                    P_(("z", i, ti, g), [(kview(8, 512), wsrc(w_in, 0, 8, g * 512, 512))])
                for s_ in range(4):
                    P_(("wout", i, ti, s_), [(kview(16, 256), wsrc(I["ssm_w_out"][j], 0, 16, s_ * 256, 256))])
            else:
                wq = I["att_w_qkv"][j]
                for s_ in range(6):
                    P_(("qkv", i, ti, s_), [(kview(8, 384), wsrc(wq, 0, 8, s_ * 384, 384))])
                for s_ in range(4):
                    P_(("wo", i, ti, s_), [(lambda s: s[0:64, 0:12 * 256].rearrange("p (k n) -> p k n", k=12),
                                            I["att_w_o"][j][:, s_ * 256:(s_ + 1) * 256].rearrange("(k p) n -> p k n", p=64))])

    def plan_xa(i):
        for q in range(4):
            P_(("kv", i, q), [(kview(8, 512), wsrc(I["xa_w_kv"][i], 0, 8, q * 512, 512))])
        for ti in range(4):
            for q in range(2):
                P_(("xq", i, ti, q), [(kview(8, 512), wsrc(I["xa_w_q"][i], 0, 8, q * 512, 512))])
            for q in range(2):
                P_(("xo", i, ti, q), [(kview(8, 512), wsrc(I["xa_w_o"][i], 0, 8, q * 512, 512))])

    def plan_ffn(i):
        gu = I["ffn_w_gu"][i]
        for ti in range(4):
            for s_ in range(11):
                P_(("gu", i, ti, s_), [(kview(8, 256, 0), wsrc(gu, 0, 8, s_ * 256, 256)),
                                       (kview(8, 256, 2048), wsrc(gu, 0, 8, DFF + s_ * 256, 256))])
            for mp in range(4):
                for hf in range(2):
                    P_(("dn", i, ti, mp, hf), [(kview(11, 256), wsrc(I["ffn_w_down"][i], hf * 11 * 128, 11, mp * 256, 256))])

    stages = []
    for i in range(DEPTH):
        stages += [("mix", i), ("xa", i), ("ffn", i)]
    if stop_after is not None:
        stages = stages[:stop_after]
    stages = [s for s in stages if s[0] != "mix" or MIXERS_ON]
    for kind, i in stages:
        {"mix": plan_mix, "xa": plan_xa, "ffn": plan_ffn}[kind](i)

    wst = {"issued": 0, "taken": 0}

    def take(tag):
        idx = wst["taken"]
        assert plan[idx][0] == tag, (plan[idx][0], tag)
        while wst["issued"] < min(len(plan), idx + NSLOT - 1):
            k = wst["issued"]
            sl = slots[k % NSLOT]
            for vf, src in plan[k][1]:
                S.dma("pool", vf(sl.t), src, writes=[sl])
            wst["issued"] += 1
        wst["taken"] += 1
        return slots[idx % NSLOT]

    def rstd_from(ps, w):
        v = RS.get()
        act(v[:, :w], ps[:, :w], AF.Ln, [ps], [v], scale=1.0 / D, bias=EPS)
        r = RS.get()
        act(r[:, :w], v[:, :w], AF.Exp, [v], [r], scale=-0.5)
        return r

    def prenorm(ti, gi):
        bg_drain()
        U = URING.get()
        for st in TILES[ti]:
            c0, w, lc = st
            ps = PSX.get()
            for kc in range(8):
                sq = BT.get()
                act(sq[:, :w], xT[:, kc, c0:c0 + w], AF.Square, [XT[ti]], [sq])
                mm(ps[:, :w], onesb[:], sq[:, :w], kc == 0, kc == 7, [onesb, sq], ps)
            r = rstd_from(ps, w)
            for kc in range(8):
                stt(U[:, kc, lc:lc + w], xT[:, kc, c0:c0 + w], GT[:, kc, gi:gi + 1], r[:, :w], ALU.mult, ALU.mult,
                    [XT[ti], GT, r], [U])
        return U

    xst = {"k": -1, "U": None}
    GI_OF = {"mix": 0, "xa": 2, "ffn": 4}

    def hoist_next_stage():
        k = xst["k"] + 1
        if HOIST and k < len(stages):
            kind, i2 = stages[k]
            xst["U"] = prenorm(0, i2 * 6 + GI_OF[kind])

    def first_u(gi):
        u = xst["U"]
        xst["U"] = None
        return u if u is not None else prenorm(0, gi)

    def post_begin(ti):
        bg_drain()
        return {st: PSX.get() for st in TILES[ti]}

    pend = []

    def post_flush():
        while pend:
            sq, mc, st, SQ, w = pend.pop(0)
            mm(SQ[st][:, :w], onesb[:], sq[:, :w], mc == 0, mc == 7, [onesb, sq], SQ[st])

    def post_evac(ps, mc, st, SQ):
        c0, w, lc = st
        act(FB[:, mc, lc:lc + w], ps[:, :w], AF.Copy, [ps], [FB])
        sq = BT.get()
        act(sq[:, :w], ps[:, :w], AF.Square, [ps], [sq])
        post_flush()
        pend.append((sq, mc, st, SQ, w))
        if not DEFER:
            post_flush()

    bgq = []

    def bg_step(n=1):
        for _ in range(n):
            if bgq:
                bgq.pop(0)()

    def bg_drain():
        while bgq:
            bgq.pop(0)()

    def post_finish(ti, gi, SQ):
        post_flush()
        bg_drain()
        for st in TILES[ti]:
            c0, w, lc = st
            v = RS.get()
            act(v[:, :w], SQ[st][:, :w], AF.Ln, [SQ[st]], [v], scale=1.0 / D, bias=EPS)
            r = RPOST.get() if BGON else RS.get()
            act(r[:, :w], v[:, :w], AF.Exp, [v], [r], scale=-0.5)
            for mc in range(8):
                def upd(mc=mc, c0=c0, w=w, lc=lc, r=r):
                    tm = TMP.get()
                    stt(tm[:, :w], FB[:, mc, lc:lc + w], GT[:, mc, gi:gi + 1], r[:, :w], ALU.mult, ALU.mult,
                        [FB, GT, r], [tm])
                    tt("pool", xT[:, mc, c0:c0 + w], xT[:, mc, c0:c0 + w], tm[:, :w], ALU.add, [XT[ti], tm], [XT[ti]])
                if BGON and ti < 3:
                    bgq.append(upd)
                else:
                    upd()

    RAW = ACC = TH = TMP
    hv = b24[:, :].rearrange("p (k n) -> p k n", k=24)

    def hbuf(jc):
        return B24[jc // 8]

    def ffn(i):
        areset()
        GH = aalloc("gh", [128, NFF, 2], F32)
        SGH = aalloc("sgh", [128, NFF, 8], F32)
        GNEW = aalloc("gnew", [128, NFF, NS], F32)
        mset("pool", GH[:], 0.0, [GH])
        for b in range(6):
            st_ = STG.get()
            j0 = b * 4
            jn = min(4, NFF - j0)
            S.dma("sp", st_[0:8, 0:jn * 128], I["st_ffn"][i].rearrange("n r f -> (n r) f")[:, j0 * 128:(j0 + jn) * 128],
                  writes=[st_])
            ps = PST.get()
            for q in range(jn):
                trp(ps[:, q * 8:(q + 1) * 8], st_[0:8, q * 128:(q + 1) * 128], identf[0:8, 0:8], [st_, identf], ps)
            cp("dve", SGH[:, j0:j0 + jn, :], ps[:, 0:jn * 8].rearrange("p (g r) -> p g r", r=8), [ps], [SGH])
        S.dma("sp", O["o_sffn"][i][:, 0, :], I["st_ffn"][i][:, 1, :], key="outc")
        for ti in range(4):
            U = Unext if (ti > 0 and HOIST) else (first_u(i * 6 + 4) if ti == 0 else prenorm(ti, i * 6 + 4))
            for s_ in range(11):
                sl = take(("gu", i, ti, s_))
                sv = sl[:, :].rearrange("p (a k n) -> p a k n", a=2, k=8)
                for jj in range(2):
                    bg_step()
                    jc = 2 * s_ + jj
                    w0 = FCW[:, jc, i * 3 + 0:i * 3 + 1]
                    w1 = FCW[:, jc, i * 3 + 1:i * 3 + 2]
                    w2 = FCW[:, jc, i * 3 + 2:i * 3 + 3]
                    bb = FCW[:, jc, 12 + i:13 + i]
                    for st in TILES[ti]:
                        c0, w, lc = st
                        pg = PSM.get()
                        for kc in range(8):
                            mm(pg[:, :w], sv[:, 0, kc, jj * 128:(jj + 1) * 128], U[:, kc, lc:lc + w], kc == 0, kc == 7,
                               [sl, U], pg)
                        pu = PSM.get()
                        for kc in range(8):
                            mm(pu[:, :w], sv[:, 1, kc, jj * 128:(jj + 1) * 128], U[:, kc, lc:lc + w], kc == 0, kc == 7,
                               [sl, U], pu)
                        ac = ACC.get()
                        if w > NS:
                            raw = RAW.get()
                            cp("pool", raw[:, 0:2], GH[:, jc, :], [GH], [raw])
                            act(raw[:, 2:2 + w], pg[:, :w], AF.Copy, [pg], [raw])
                            cp("pool", GH[:, jc, :], raw[:, w:w + 2], [raw], [GH])
                            act(ac[:, :w], raw[:, 0:w], AF.Identity, [raw, FCW], [ac], scale=w0, bias=bb)
                            stt(ac[:, :w], raw[:, 1:1 + w], w1, ac[:, :w], ALU.mult, ALU.add, [raw, FCW, ac], [ac])
                            stt(ac[:, :w], pg[:, :w], w2, ac[:, :w], ALU.mult, ALU.add, [pg, FCW, ac], [ac])
                        else:
                            act(ac[:, :w], SGH[:, jc, 0:8:2], AF.Identity, [SGH, FCW], [ac], scale=w0, bias=bb)
                            stt(ac[:, :w], SGH[:, jc, 1:8:2], w1, ac[:, :w], ALU.mult, ALU.add, [SGH, FCW, ac], [ac])
                            stt(ac[:, :w], pg[:, :w], w2, ac[:, :w], ALU.mult, ALU.add, [pg, FCW, ac], [ac])
                            act(GNEW[:, jc, :], pg[:, :w], AF.Copy, [pg], [GNEW])
                        th = TH.get()
                        act(th[:, :w], ac[:, :w], AF.Tanh, [ac], [th])
                        stt(ac[:, :w], th[:, :w], 1.0, ac[:, :w], ALU.add, ALU.mult, [th, ac], [ac])
                        tt("dve", hv[:, jc, lc:lc + w], ac[:, :w], pu[:, :w], ALU.mult, [ac, pu], [hbuf(jc)])
            if ti == 3:
                rows_out(lambda jc: GH[:, jc, :], [GH], 2, NFF, O["o_pffn"][i])
                rows_out(lambda jc: GNEW[:, jc, :], [GNEW], NS, NFF, O["o_sffn"][i][:, 1, :])
            Unext = prenorm(ti + 1, i * 6 + 4) if (ti < 3 and HOIST) else None
            if ti == 3:
                hoist_next_stage()
            SQ = post_begin(ti)
            for mp in range(4):
                sa = take(("dn", i, ti, mp, 0))
                sb_ = take(("dn", i, ti, mp, 1))
                va = sa[:, 0:11 * 256].rearrange("p (k n) -> p k n", k=11)
                vb = sb_[:, 0:11 * 256].rearrange("p (k n) -> p k n", k=11)
                for m2 in range(2):
                    mc = 2 * mp + m2
                    for st in TILES[ti]:
                        c0, w, lc = st
                        pf = PSM.get()
                        for jc in range(NFF):
                            src = (va if jc < 11 else vb)[:, jc % 11, m2 * 128:(m2 + 1) * 128]
                            mm(pf[:, :w], src, hv[:, jc, lc:lc + w], jc == 0, jc == NFF - 1,
                               [sa if jc < 11 else sb_, hbuf(jc)], pf)
                        post_evac(pf, mc, st, SQ)
            post_finish(ti, i * 6 + 5, SQ)

    qv = b24[:, 0:8 * TW].rearrange("p (k n) -> p k n", k=8)
    ov = b24[:, 8 * TW:16 * TW].rearrange("p (k n) -> p k n", k=8)

    def make_kt(dst, src):
        for h4 in range(2):
            ps = PST.get()
            pb = ps[:, :].bitcast(BF16)
            for q in range(4):
                hc = h4 * 4 + q
                for mb in range(2):
                    o0 = (q * 2 + mb) * 128
                    trp(pb[:, o0:o0 + 128], src[:, mb, hc * 128:(hc + 1) * 128], identb[:], [src, identb], ps)
            cp("act", dst[:, h4 * 4:h4 * 4 + 4, :], pb[:, :].rearrange("p (q m) -> p q m", q=4), [ps], [dst])

    def xattn_a(PTR, ktb, h, w, lc):
        PTb = PTR.get()
        for mb in range(2):
            ps = PSM.get()
            for ec in range(2):
                mm(ps[:, :w], ktb[:, h * 2 + ec, mb * 128:(mb + 1) * 128], qv[:, h * 2 + ec, lc:lc + w], ec == 0, ec == 1,
                   [ktb, B24[0]], ps)
            act(PTb[:, mb, :w], ps[:, :w], AF.Exp, [ps], [PTb])
        return PTb

    def xattn_b(PTb, vbb, h, w, lc):
        pd = PSX.get()
        for mb in range(2):
            mm(pd[:, :w], onesb[:], PTb[:, mb, :w], mb == 0, mb == 1, [onesb, PTb], pd)
        rd = RS.get()
        act(rd[:, :w], pd[:, :w], AF.Ln, [pd], [rd])
        act(rd[:, :w], rd[:, :w], AF.Exp, [rd], [rd], scale=-1.0)
        for ec in range(2):
            po = PSM.get()
            for mb in range(2):
                mm(po[:, :w], vbb[:, mb, h * 256 + ec * 128:h * 256 + ec * 128 + 128], PTb[:, mb, :w], mb == 0, mb == 1,
                   [vbb, PTb], po)
            tt("dve", ov[:, h * 2 + ec, lc:lc + w], po[:, :w], rd[:, :w], ALU.mult, [po, rd], [B24[1]])

    def xattn_core(PTR, ktb, vbb, h, w, lc):
        xattn_b(xattn_a(PTR, ktb, h, w, lc), vbb, h, w, lc)

    def xattn(i):
        areset()
        MNT = aalloc("mnt", [128, 8, 256], BF16)
        KB = aalloc("kb", [128, 2, 1024], BF16)
        VB = aalloc("vb", [128, 2, 1024], BF16)
        KT = aalloc("kt", [128, 8, 256], BF16)
        SVB = aalloc("svb", [128, 2, 1024], BF16)
        SKT = MNT
        MNB = aalloc("mnb", [128, 1024], BF16)
        mg = aalloc("mg", [128, 1024], F32)
        m_ = aalloc("memblk", [128, 1024], F32)
        PTR = Ring([aalloc("ptr%d" % k, [128, 2, 512], BF16) for k in range(2)])
        S.dma("sp", mg[:, :], I["mem_norm"][i:i + 1, :].partition_broadcast(128), writes=[mg])
        for mb in range(2):
            S.dma("sp", m_[:, :], I["mem"][mb * 128:(mb + 1) * 128, :], writes=[m_])
            ss = SM.get()
            act(MNB[:, :], m_[:, :], AF.Square, [m_], [MNB, ss], accum_out=ss[:, 0:1])
            act(ss[:, 1:2], ss[:, 0:1], AF.Sqrt, [ss], [ss], scale=1.0 / D, bias=EPS)
            recip(ss[:, 2:3], ss[:, 1:2], [ss], [ss])
            S.add("dve", lambda e, ss=ss: e.scalar_tensor_tensor(out=MNB[:, :], in0=m_[:, :], scalar=ss[:, 2:3], in1=mg[:, :],
                                                                       op0=ALU.mult, op1=ALU.mult), [m_, ss, mg], [MNB], nowaw=False)
            ps = PST.get()
            pb = ps[:, :].bitcast(BF16)
            for kc in range(8):
                trp(pb[:, kc * 128:(kc + 1) * 128], MNB[:, kc * 128:(kc + 1) * 128], identb[:], [MNB, identb], ps)
            cp("act", MNT[:, :, mb * 128:(mb + 1) * 128], pb[:, :].rearrange("p (k m) -> p k m", k=8), [ps], [MNT])
        if DBG < 1:
            return
        for q in range(4):
            sl = take(("kv", i, q))
            sv = sl[:, :].rearrange("p (k n) -> p k n", k=8)
            for mb in range(2):
                if DBG2 < 1:
                    continue
                ps = PSM.get()
                for kc in range(8):
                    mm(ps[:, :], MNT[:, kc, mb * 128:(mb + 1) * 128], sv[:, kc, :], kc == 0, kc == 7, [MNT, sl], ps)
                if DBG2 < 2:
                    continue
                st_ = STG.get()
                act(st_[:, 0:512], ps[:, :], AF.Copy, [ps], [st_])
                S.dma("sp", O["o_pmem"][i][mb * 128:(mb + 1) * 128, q * 512:(q + 1) * 512], st_[:, 0:512], reads=[st_],
                      key=st_)
                if DBG2 < 3:
                    continue
                dstb = KB if q < 2 else VB
                cp("dve", dstb[:, mb, (q % 2) * 512:(q % 2) * 512 + 512], ps[:, :], [ps], [dstb])
        if DBG < 2:
            return
        make_kt(KT, KB)
        if DBG < 3:
            return
        for ti in range(4):
            if DBG < 4 and ti > 0:
                return
            U = Unext if (ti > 0 and HOIST) else (first_u(i * 6 + 2) if ti == 0 else prenorm(ti, i * 6 + 2))
            for q in range(2):
                sl = take(("xq", i, ti, q))
                sv = sl[:, :].rearrange("p (k n) -> p k n", k=8)
                for m4 in range(4):
                    bg_step()
                    hc = q * 4 + m4
                    for st in TILES[ti]:
                        c0, w, lc = st
                        ps = PSM.get()
                        for kc in range(8):
                            mm(ps[:, :w], sv[:, kc, m4 * 128:(m4 + 1) * 128], U[:, kc, lc:lc + w], kc == 0, kc == 7,
                               [sl, U], ps)
                        act(qv[:, hc, lc:lc + w], ps[:, :w], AF.Copy, [ps], [B24[0]], scale=1.0 / 16.0)
            for st in TILES[ti]:
                c0, w, lc = st
                if w > NS:
                    pts = [xattn_a(PTR, KT, 0, w, lc)]
                    for h in range(4):
                        if h < 3:
                            pts.append(xattn_a(PTR, KT, h + 1, w, lc))
                        xattn_b(pts[h], VB, h, w, lc)
                else:
                    for n in range(NS):
                        cm = I["cmem"][i][n]
                        S.dma("pool", KB[:, :, :], cm[:, 0:1024].rearrange("(mb p) f -> p mb f", p=128), writes=[KB])
                        S.dma("pool", SVB[:, :, :], cm[:, 1024:2048].rearrange("(mb p) f -> p mb f", p=128), writes=[SVB])
                        make_kt(SKT, KB)
                        for h in range(4):
                            xattn_core(PTR, SKT, SVB, h, 1, lc + n)
            Unext = prenorm(ti + 1, i * 6 + 2) if (ti < 3 and HOIST) else None
            if ti == 3:
                hoist_next_stage()
            SQ = post_begin(ti)
            for q in range(2):
                sl = take(("xo", i, ti, q))
                sv = sl[:, :].rearrange("p (k n) -> p k n", k=8)
                for m4 in range(4):
                    mc = q * 4 + m4
                    for st in TILES[ti]:
                        c0, w, lc = st
                        pf = PSM.get()
                        for kc in range(8):
                            mm(pf[:, :w], sv[:, kc, m4 * 128:(m4 + 1) * 128], ov[:, kc, lc:lc + w], kc == 0, kc == 7,
                               [sl, B24[1]], pf)
                        post_evac(pf, mc, st, SQ)
            post_finish(ti, i * 6 + 3, SQ)

    def ssd(i):
        j = i // 2
        areset()
        HT32 = aalloc("ht32", [128, 2048], F32)
        HTB = aalloc("htb", [128, 2048], BF16)
        DF = aalloc("df", [128, 16], F32)
        XDT = aalloc("xdt", [128, 2048], BF16)
        YTM = aalloc("ytm", [128, 2048], BF16)
        BTM = aalloc("btm", [128, 512], BF16)
        CBM = [aalloc("cbm%d" % k, [128, 128], F32) for k in range(2)]
        XH = aalloc("xh", [128, 24, 3], F32)
        XHB = aalloc("xhb", [128, 24, 3], BF16)
        RAWB = Ring([aalloc("rawb%d" % k, [128, TW + 4], BF16) for k in range(3)])
        DGR = Ring([aalloc("dg%d" % k, [128, 4, 128], BF16) for k in range(2)])
        SXH = aalloc("sxh", [128, 24, 12], F32)
        XNEW = aalloc("xnew", [128, 24, NS], F32)
        WDT = aalloc("wdt", [128, 8, 32], BF16)
        PRM = aalloc("prm", [128, 3, 32], F32)
        DTA = aalloc("dta", [128, 5, 64], F32)
        EE = aalloc("ee", [128, 128], F32)
        xact = hv
        w_in = I["ssm_w_in"][j]
        S.dma("pool", WDT[:, :, :], w_in[:, DI + CONV:INW].rearrange("(k p) n -> p k n", p=128), writes=[WDT])
        S.dma("sp", PRM[:, 0, :], I["ssm_dt_bias"][j:j + 1, :].partition_broadcast(128), writes=[PRM])
        S.dma("sp", PRM[:, 1, :], I["ssm_a_log"][j:j + 1, :].partition_broadcast(128), writes=[PRM])
        S.dma("sp", PRM[:, 2, :], I["ssm_d"][j:j + 1, :].partition_broadcast(128), writes=[PRM])
        S.dma("sp", DF[0:64, :], I["ssm_d"][j:j + 1, 0:32:2].partition_broadcast(64), writes=[DF], allow_slow_non_contiguous=True)
        S.dma("sp", DF[64:128, :], I["ssm_d"][j:j + 1, 1:32:2].partition_broadcast(64), writes=[DF], allow_slow_non_contiguous=True)
        act(PRM[:, 1, :], PRM[:, 1, :], AF.Exp, [PRM], [PRM])
        tsc("dve", PRM[:, 1, :], PRM[:, 1, :], -1.0, None, ALU.mult, None, [PRM], [PRM])
        mset("pool", XH[:], 0.0, [XH])
        mset("pool", XHB[:], 0.0, [XHB])
        mset("pool", HT32[:], 0.0, [HT32])
        mset("pool", HTB[:], 0.0, [HTB])
        for b in range(6):
            st_ = STG.get()
            S.dma("sp", st_[0:12, 0:512], I["st_conv"][j].rearrange("n r f -> (n r) f")[:, b * 512:(b + 1) * 512], writes=[st_])
            ps = PST.get()
            for q in range(4):
                trp(ps[:, q * 12:(q + 1) * 12], st_[0:12, q * 128:(q + 1) * 128], identf[0:12, 0:12], [st_, identf], ps)
            cp("dve", SXH[:, b * 4:b * 4 + 4, :], ps[:, 0:48].rearrange("p (g r) -> p g r", r=12), [ps], [SXH])

        def state_out(dst):
            for q4 in range(4):
                st_ = STG.get()
                ps = PST.get()
                for q in range(4):
                    c_ = q4 * 4 + q
                    trp(ps[:, q * 128:(q + 1) * 128], HT32[:, c_ * 128:(c_ + 1) * 128], identf[:], [HT32, identf], ps)
                cp("act", st_[:, 0:512], ps[:, :], [ps], [st_])
                S.dma("sp", dst[q4 * 512:(q4 + 1) * 512, :].rearrange("(c p) k -> p c k", p=128),
                      st_[:, 0:512].rearrange("p (c k) -> p c k", c=4), reads=[st_], key=st_)

        for ti in range(4):
            U = Unext if (ti > 0 and HOIST) else (first_u(i * 6 + 0) if ti == 0 else prenorm(ti, i * 6 + 0))
            later = []
            for s_ in range(6):
                sl = take(("xbc", i, ti, s_))
                sv = sl[:, :].rearrange("p (k n) -> p k n", k=8)
                for c4 in range(4):
                    bg_step()
                    cc = s_ * 4 + c4
                    w0 = SCW[:, cc, j * 4 + 0:j * 4 + 1]
                    w1 = SCW[:, cc, j * 4 + 1:j * 4 + 2]
                    w2 = SCW[:, cc, j * 4 + 2:j * 4 + 3]
                    w3 = SCW[:, cc, j * 4 + 3:j * 4 + 4]
                    bb = SCW[:, cc, 8 + j:9 + j]
                    for st in TILES[ti]:
                        c0, w, lc = st
                        ps = PSM.get()
                        for kc in range(8):
                            mm(ps[:, :w], sv[:, kc, c4 * 128:(c4 + 1) * 128], U[:, kc, lc:lc + w], kc == 0, kc == 7, [sl, U], ps)
                        if w > NS:
                            dg = DGR.get()
                            tt("dve", dg[:, :, :], identb[:, :].unsqueeze(1).to_broadcast([128, 4, 128]),
                               SCW[:, cc, j * 4:j * 4 + 4].unsqueeze(2).to_broadcast([128, 4, 128]), ALU.mult, [identb, SCW], [dg])
                            raw = RAWB.get()
                            cp("pool", raw[:, 0:3], XHB[:, cc, :], [XHB], [raw])
                            act(raw[:, 3:3 + w], ps[:, :w], AF.Copy, [ps], [raw])
                            cp("pool", XHB[:, cc, :], raw[:, w:w + 3], [raw], [XHB])
                            if ti == 3:
                                act(XH[:, cc, :], ps[:, w - 3:w], AF.Copy, [ps], [XH])

                            def fin(raw=raw, cc=cc, w=w, lc=lc, bb=bb, dg=dg):
                                pc2 = PSM.get()
                                for k in range(4):
                                    mm(pc2[:, :w], dg[:, k, :], raw[:, k:k + w], k == 0, k == 3, [dg, raw], pc2)
                                ac = TMP.get()
                                act(ac[:, :w], pc2[:, :w], AF.Identity, [pc2, SCW], [ac], bias=bb)
                                th = TMP.get()
                                act(th[:, :w], ac[:, :w], AF.Tanh, [ac], [th])
                                stt(xact[:, cc, lc:lc + w], th[:, :w], 1.0, ac[:, :w], ALU.add, ALU.mult, [th, ac], [hbuf(cc)])
                            while later:
                                later.pop(0)()
                            later.append(fin)
                        else:
                            ac = TMP.get()
                            act(ac[:, :w], SXH[:, cc, 0:12:3], AF.Identity, [SXH, SCW], [ac], scale=w0, bias=bb)
                            stt(ac[:, :w], SXH[:, cc, 1:12:3], w1, ac[:, :w], ALU.mult, ALU.add, [SXH, SCW, ac], [ac])
                            stt(ac[:, :w], SXH[:, cc, 2:12:3], w2, ac[:, :w], ALU.mult, ALU.add, [SXH, SCW, ac], [ac])
                            stt(ac[:, :w], ps[:, :w], w3, ac[:, :w], ALU.mult, ALU.add, [ps, SCW, ac], [ac])
                            act(XNEW[:, cc, :], ps[:, :w], AF.Copy, [ps], [XNEW])
                            th = TMP.get()
                            act(th[:, :w], ac[:, :w], AF.Tanh, [ac], [th])
                            stt(xact[:, cc, lc:lc + w], th[:, :w], 1.0, ac[:, :w], ALU.add, ALU.mult, [th, ac], [hbuf(cc)])
            while later:
                later.pop(0)()
            if ti == 3:
                rows_out(lambda cc: XH[:, cc, :], [XH], 3, 24, O["o_pconv"][j])
                rows_out(lambda cc: XNEW[:, cc, :], [XNEW], NS, 24, O["o_sconv"][j][:, 2, :])
            blocks = [(c * 128, 128) for c in range(4)] + ([(512, NS)] if ti == 3 else [])
            for bi, (lcb, L) in enumerate(blocks):
                pd = PSX.get()
                for kc in range(8):
                    mm(pd[0:L, 0:32], U[:, kc, lcb:lcb + L], WDT[:, kc, :], kc == 0, kc == 7, [U, WDT], pd)
                tt("dve", DTA[0:L, bi, 0:32], pd[0:L, 0:32], PRM[0:L, 0, :], ALU.add, [pd, PRM], [DTA])
                act(DTA[0:L, bi, 0:32], DTA[0:L, bi, 0:32], AF.Exp, [DTA], [DTA])
                act(DTA[0:L, bi, 0:32], DTA[0:L, bi, 0:32], AF.Ln, [DTA], [DTA], bias=1.0)
                tt("dve", DTA[0:L, bi, 32:64], DTA[0:L, bi, 0:32], PRM[0:L, 1, :], ALU.mult, [DTA, PRM], [DTA])
            for bi, (lcb, L) in enumerate(blocks):
                dtv = DTA[0:L, bi, 0:32]
                av = DTA[0:L, bi, 32:64]
                for hf in range(2):
                    ps = PST.get()
                    pb = ps[:, :].bitcast(BF16)
                    for q in range(8):
                        cc = hf * 8 + q
                        trp(pb[0:L, q * 128:(q + 1) * 128], xact[:, cc, lcb:lcb + L], identb[:], [hbuf(cc), identb], ps)
                    tt("dve", XDT[0:L, hf * 1024:(hf + 1) * 1024].rearrange("p (h e) -> p h e", e=64),
                       pb[0:L, :].rearrange("p (h e) -> p h e", e=64),
                       dtv[:, hf * 16:(hf + 1) * 16].unsqueeze(2).to_broadcast([L, 16, 64]), ALU.mult, [ps, DTA], [XDT])
                ps = PST.get()
                pb = ps[:, :].bitcast(BF16)
                for q in range(4):
                    trp(pb[0:L, q * 128:(q + 1) * 128], xact[:, 16 + q, lcb:lcb + L], identb[:], [B24[2], identb], ps)
                cp("act", BTM[0:L, :], pb[0:L, 0:512], [ps], [BTM])
                if L == 128:
                    pa = PSX.get()
                    mm(pa[:, 0:32], trile_f[:], av, True, True, [trile_f, DTA], pa)
                    mm(pa[:, 32:64], onesf[:], av, True, True, [onesf, DTA], pa)
                    act(EE[:, 0:32], pa[:, 0:32], AF.Copy, [pa], [EE])
                    act(EE[:, 32:64], pa[:, 0:32], AF.Exp, [pa], [EE])
                    tt("dve", EE[:, 64:96], pa[:, 32:64], EE[:, 0:32], ALU.subtract, [pa, EE], [EE])
                    act(EE[:, 64:96], EE[:, 64:96], AF.Exp, [EE], [EE])
                    act(EE[:, 96:128], pa[:, 32:64], AF.Exp, [pa], [EE])
                    LTs, PSEGs, WTs, PYs = {}, {}, {}, {}

                    def sP(g):
                        pc = PSX.get()
                        mm(pc[:, 0:128], xact[:, 16 + g, lcb:lcb + 128], xact[:, 20 + g, lcb:lcb + 128], True, True, [B24[2]], pc)
                        tt("dve", CBM[g % 2][:, :], pc[:, 0:128], trile_f[:], ALU.mult, [pc, trile_f], [CBM[g % 2]])

                    def s1(n):
                        g, hf = divmod(n, 2)
                        h0 = g * 8 + hf * 4
                        Lt = TMP.get()
                        tt("pool", Lt[:, 0:512].rearrange("p (h s) -> p h s", h=4),
                           maskgt_f[:, :].unsqueeze(1).to_broadcast([128, 4, 128]),
                           av[:, h0:h0 + 4].unsqueeze(2).to_broadcast([128, 4, 128]), ALU.mult, [maskgt_f, DTA], [Lt])
                        LTs[n] = Lt

                    def s2(n):
                        g, hf = divmod(n, 2)
                        if hf == 0:
                            sP(g)
                        Lt = LTs[n]
                        pseg = PSM.get()
                        for h4 in range(4):
                            mm(pseg[:, h4 * 128:(h4 + 1) * 128], Lt[:, h4 * 128:(h4 + 1) * 128], trile_f[:], True, True,
                               [Lt, trile_f], pseg)
                        PSEGs[n] = pseg

                    def s3(n):
                        g, hf = divmod(n, 2)
                        dec = TMP.get()
                        act(dec[:, 0:512], PSEGs[n][:, :], AF.Exp, [PSEGs[n]], [dec])
                        WT = BT.get()
                        tt("dve", WT[:, 0:512].rearrange("p (h t) -> p h t", h=4),
                           dec[:, 0:512].rearrange("p (h t) -> p h t", h=4),
                           CBM[g % 2][:, :].unsqueeze(1).to_broadcast([128, 4, 128]), ALU.mult, [dec, CBM[g % 2]], [WT])
                        WTs[n] = WT

                    def s4(n):
                        g, hf = divmod(n, 2)
                        if hf == 0:
                            PYs[g] = PST.get()
                        py = PYs[g]
                        WT = WTs[n]
                        for h4 in range(4):
                            h = g * 8 + hf * 4 + h4
                            mm(py[:, (hf * 4 + h4) * 64:(hf * 4 + h4 + 1) * 64], WT[:, h4 * 128:(h4 + 1) * 128],
                               XDT[:, h * 64:(h + 1) * 64], True, True, [WT, XDT], py)

                    def sE(g):
                        gc = slice(g * 512, (g + 1) * 512)
                        py = PYs[g]
                        pyi = PSX.get()
                        mm(pyi[:, :], xact[:, 20 + g, lcb:lcb + 128], HTB[:, gc], True, True, [B24[2], HTB], pyi)
                        yt = TMP.get()
                        tt("dve", yt[:, 0:512].rearrange("p (h e) -> p h e", e=64), pyi[:, :].rearrange("p (h e) -> p h e", e=64),
                           EE[:, 32 + g * 8:32 + g * 8 + 8].unsqueeze(2).to_broadcast([128, 8, 64]), ALU.mult, [pyi, EE], [yt])
                        tt("dve", YTM[:, gc], yt[:, 0:512], py[:, :], ALU.add, [yt, py], [YTM])
                        xd = BT.get()
                        tt("pool", xd[:, 0:512].rearrange("p (h e) -> p h e", e=64), XDT[:, gc].rearrange("p (h e) -> p h e", e=64),
                           EE[:, 64 + g * 8:64 + g * 8 + 8].unsqueeze(2).to_broadcast([128, 8, 64]), ALU.mult, [XDT, EE], [xd])
                        pn = PSX.get()
                        mm(pn[:, :], BTM[:, g * 128:(g + 1) * 128], xd[:, 0:512], True, True, [BTM, xd], pn)
                        tt("pool", HT32[:, gc].rearrange("p (h e) -> p h e", e=64), HT32[:, gc].rearrange("p (h e) -> p h e", e=64),
                           EE[:, 96 + g * 8:96 + g * 8 + 8].unsqueeze(2).to_broadcast([128, 8, 64]), ALU.mult, [HT32, EE], [HT32])
                        tt("dve", HT32[:, gc], HT32[:, gc], pn[:, :], ALU.add, [HT32, pn], [HT32])
                        cp("act", HTB[:, gc], HT32[:, gc], [HT32], [HTB])

                    s1(0)
                    s1(1)
                    s2(0)
                    for n in range(8):
                        if n + 2 < 8:
                            s1(n + 2)
                        if n + 1 < 8:
                            s2(n + 1)
                        s3(n)
                        s4(n)
                        if n % 2 == 1:
                            sE(n // 2)
                    if ti == 3 and bi == 3:
                        state_out(O["o_pssm"][j])
                else:
                    mset("pool", YTM[0:NS, :], 0.0, [YTM])
                    for n in range(NS):
                        oh = identf[0:NS, n:n + 1]
                        for q4 in range(4):
                            st_ = STG.get()
                            S.dma("sp", st_[:, 0:512].rearrange("p (c k) -> p c k", c=4),
                                  I["st_ssm"][j][n][q4 * 512:(q4 + 1) * 512, :].rearrange("(c p) k -> p c k", p=128), writes=[st_])
                            ps = PST.get()
                            for q in range(4):
                                trp(ps[:, q * 128:(q + 1) * 128], st_[:, q * 128:(q + 1) * 128], identf[:], [st_, identf], ps)
                            cp("act", HT32[:, q4 * 512:(q4 + 1) * 512], ps[:, :], [ps], [HT32])
                        pa = PSX.get()
                        mm(pa[:, 0:32], identf[0:NS, n:n + 1].to_broadcast([NS, 128]), av, True, True, [identf, DTA], pa)
                        act(EE[:, 0:32], pa[:, 0:32], AF.Exp, [pa], [EE])
                        tt("dve", HT32[:, :].rearrange("p (h e) -> p h e", e=64), HT32[:, :].rearrange("p (h e) -> p h e", e=64),
                           EE[:, 0:32].unsqueeze(2).to_broadcast([128, 32, 64]), ALU.mult, [HT32, EE], [HT32])
                        for g in range(4):
                            gc = slice(g * 512, (g + 1) * 512)
                            xd = BT.get()
                            tsc("dve", xd[0:NS, 0:512], XDT[0:NS, gc], oh, None, ALU.mult, None, [XDT, identf], [xd])
                            pn = PSM.get()
                            mm(pn[:, :], BTM[0:NS, g * 128:(g + 1) * 128], xd[0:NS, 0:512], True, True, [BTM, xd], pn)
                            tt("dve", HT32[:, gc], HT32[:, gc], pn[:, :], ALU.add, [HT32, pn], [HT32])
                            cp("act", HTB[:, gc], HT32[:, gc], [HT32], [HTB])
                            py = PSM.get()
                            mm(py[0:NS, :], xact[:, 20 + g, lcb:lcb + NS], HTB[:, gc], True, True, [B24[2], HTB], py)
                            stt(YTM[0:NS, gc], py[0:NS, :], oh, YTM[0:NS, gc], ALU.mult, ALU.add, [py, identf, YTM], [YTM])
                        state_out(O["o_sssm"][j][n])
                for hf in range(2):
                    ps = PST.get()
                    pb = ps[:, :].bitcast(BF16)
                    for q in range(8):
                        fc = hf * 8 + q
                        trp(pb[:, q * L:(q + 1) * L], YTM[0:L, fc * 128:(fc + 1) * 128], identb[0:L, 0:L], [YTM, identb], ps)
                    for q in range(8):
                        fc = hf * 8 + q
                        stt(hv[:, fc, lcb:lcb + L], hv[:, fc, lcb:lcb + L], DF[:, fc:fc + 1], pb[:, q * L:(q + 1) * L], ALU.mult, ALU.add,
                            [B24[hf], DF, ps], [B24[hf]])
            zp = []
            zpend = []

            def zflush():
                while zp:
                    pb_, sq_, w_, c4_ = zp.pop(0)
                    mm(pb_[:, :w_], onesb[:], sq_[:, :w_], c4_ == 0, c4_ == 3, [onesb, sq_], pb_)

            for g in range(4):
                sl = take(("z", i, ti, g))
                sv = sl[:, :].rearrange("p (k n) -> p k n", k=8)
                pss = {st: PSX.get() for st in TILES[ti]}
                for c4 in range(4):
                    fc = g * 4 + c4
                    for st in TILES[ti]:
                        c0, w, lc = st
                        pz = PSM.get()
                        for kc in range(8):
                            mm(pz[:, :w], sv[:, kc, c4 * 128:(c4 + 1) * 128], U[:, kc, lc:lc + w], kc == 0, kc == 7, [sl, U], pz)
                        th = TMP.get()
                        act(th[:, :w], pz[:, :w], AF.Tanh, [pz], [th], scale=0.5)
                        stt(th[:, :w], th[:, :w], 1.0, pz[:, :w], ALU.add, ALU.mult, [th, pz], [th])
                        stt(hv[:, fc, lc:lc + w], th[:, :w], 0.5, hv[:, fc, lc:lc + w], ALU.mult, ALU.mult, [th, hbuf(fc)], [hbuf(fc)])
                        sq = BT.get()
                        act(sq[:, :w], hv[:, fc, lc:lc + w], AF.Square, [hbuf(fc)], [sq])
                        zflush()
                        zp.append((pss[st], sq, w, c4))
                zflush()

                def zfin(g=g, pss=pss):
                    for st in TILES[ti]:
                        c0, w, lc = st
                        v = RS.get()
                        act(v[:, :w], pss[st][:, :w], AF.Ln, [pss[st]], [v], scale=1.0 / 512.0, bias=EPS)
                        r = RS.get()
                        act(r[:, :w], v[:, :w], AF.Exp, [v], [r], scale=-0.5)
                        for c4 in range(4):
                            fc = g * 4 + c4
                            stt(hv[:, fc, lc:lc + w], hv[:, fc, lc:lc + w], SNW[:, fc, j:j + 1], r[:, :w], ALU.mult, ALU.mult,
                                [hbuf(fc), SNW, r], [hbuf(fc)])
                if len(TILES[ti]) == 1:
                    while zpend:
                        zpend.pop(0)()
                    zpend.append(zfin)
                else:
                    zfin()
            while zpend:
                zpend.pop(0)()
            Unext = prenorm(ti + 1, i * 6 + 0) if (ti < 3 and HOIST) else None
            if ti == 3:
                hoist_next_stage()
            SQ = post_begin(ti)
            for s_ in range(4):
                sl = take(("wout", i, ti, s_))
                sv = sl[:, :].rearrange("p (k n) -> p k n", k=16)
                for m2 in range(2):
                    mc = s_ * 2 + m2
                    for st in TILES[ti]:
                        c0, w, lc = st
                        pf = PSM.get()
                        for kc in range(16):
                            mm(pf[:, :w], sv[:, kc, m2 * 128:(m2 + 1) * 128], hv[:, kc, lc:lc + w], kc == 0, kc == 15,
                               [sl, hbuf(kc)], pf)
                        post_evac(pf, mc, st, SQ)
            post_finish(ti, i * 6 + 1, SQ)

    def att(i):
        j = i // 2
        areset()
        QT = aalloc("qt", [128, 6, TW], BF16)
        OB = aalloc("ob", [128, 12, TW], BF16)
        DENT = aalloc("dent", [128, 4, TW], F32)
        PTA = Ring([aalloc("pta%d" % k, [128, 512], BF16) for k in range(4)])
        VA = Ring([aalloc("va%d" % k, [128, 4, 64], BF16) for k in range(11)])
        KTA = b24[:, 0:6 * T].rearrange("p (k n) -> p k n", k=6)
        BALL = [B24[0], B24[1], B24[2]]
        caches = (("128", 128, 1), ("512", 512, 4), ("2048", 2048, 16))

        def kv_out(tm, L, colbase, slab, sample):
            kvoff = 0 if slab < 4 else 256
            lo = slab % 2
            pieces = []
            if lo == 0:
                pieces += [(0, 0, 256, 0), (1, 256, 128, 0)]
            else:
                pieces += [(1, 0, 128, 128), (2, 128, 256, 0)]
            for g, tc, ncol, oc in pieces:
                nm, W, dil = caches[g]
                if sample:
                    S.dma("sp", O["o_s" + nm][j][:, W - 1, kvoff + oc:kvoff + oc + ncol], tm[0:L, tc:tc + ncol], reads=[tm], key=tm)
                else:
                    r0 = colbase - (TP - W)
                    if r0 >= 0:
                        S.dma("sp", O["o_p" + nm][j][r0:r0 + L, kvoff + oc:kvoff + oc + ncol], tm[0:L, tc:tc + ncol], reads=[tm], key=tm)

        def unit_a(g, qsl, nq, kbs):
            PTs = []
            for kt_fn, nk, va, mask, rk in kbs:
                ps = PSM.get()
                for hh in (0, 2, 1, 3):
                    H = g * 4 + hh
                    pb_ = 64 * (H % 2)
                    mm(ps[0:nk, hh * nq:(hh + 1) * nq], kt_fn(hh), QT[pb_:pb_ + 64, H // 2, qsl], True, True, rk + [QT], ps, rg=pb_)
                PT = PTA.get()
                act(PT[0:nk, 0:4 * nq], ps[0:nk, 0:4 * nq], AF.Exp, [ps], [PT])
                if mask is not None:
                    tt("dve", PT[0:nk, 0:4 * nq].rearrange("p (h q) -> p h q", h=4), PT[0:nk, 0:4 * nq].rearrange("p (h q) -> p h q", h=4),
                       mask.unsqueeze(1).to_broadcast([nk, 4, nq]), ALU.mult, [PT, trile_b, trige_b, identb], [PT])
                PTs.append(PT)
            return PTs

        def unit_b(g, nq, kbs, PTs, osl):
            po = PSM.get()
            for hh in range(4):
                for bi, (kt_fn, nk, va, mask, rk) in enumerate(kbs):
                    mm(po[0:64, hh * nq:(hh + 1) * nq], va[0:nk, hh, :], PTs[bi][0:nk, hh * nq:(hh + 1) * nq], bi == 0,
                       bi == len(kbs) - 1, [va, PTs[bi]], po)
            pdn = PSX.get()
            for bi, (kt_fn, nk, va, mask, rk) in enumerate(kbs):
                mm(pdn[0:64, 0:4 * nq], onesb[0:nk, 0:64], PTs[bi][0:nk, 0:4 * nq], bi == 0, bi == len(kbs) - 1, [onesb, PTs[bi]], pdn)
            cp("act", OB[0:64, g * 4:g * 4 + 4, osl], po[0:64, 0:4 * nq].rearrange("p (h q) -> p h q", h=4), [po], [OB])
            if g == 0:
                cp("dve", DENT[0:64, :, osl], pdn[0:64, 0:4 * nq].rearrange("p (h q) -> p h q", h=4), [pdn], [DENT])
            else:
                tt("dve", DENT[0:64, :, osl], DENT[0:64, :, osl], pdn[0:64, 0:4 * nq].rearrange("p (h q) -> p h q", h=4), ALU.add,
                   [DENT, pdn], [DENT])

        def run_units(units):
            if not units:
                return
            cur = units[0]()
            cur_pt = unit_a(cur[0], cur[1], cur[2], cur[3])
            for n in range(len(units)):
                nxt = nxt_pt = None
                if n + 1 < len(units):
                    nxt = units[n + 1]()
                    nxt_pt = unit_a(nxt[0], nxt[1], nxt[2], nxt[3])
                unit_b(cur[0], cur[2], cur[3], cur_pt, cur[4])
                cur, cur_pt = nxt, nxt_pt

        def kt_all(g, cols):
            def f(hh):
                H = g * 4 + hh
                pb_ = 64 * (H % 2)
                return KTA[pb_:pb_ + 64, H // 2, cols]
            return f

        def load_va(rows, g, nk):
            va = VA.get()
            S.dma("pool", va[0:nk, :, :], vscr[rows, g * 256:(g + 1) * 256].rearrange("p (h e) -> p h e", e=64), reads=[VSCR], writes=[va])
            return va

        for ti in range(4):
            U = Unext if (ti > 0 and HOIST) else (first_u(i * 6 + 0) if ti == 0 else prenorm(ti, i * 6 + 0))
            t0 = ti * 512
            blocks = [(c * 128, 128, t0 + c * 128, ti * 4 + c) for c in range(4)] + ([(512, NS, TP, 16)] if ti == 3 else [])
            later = []
            for s_ in range(6):
                sl = take(("qkv", i, ti, s_))
                sv = sl[:, 0:8 * 384].rearrange("p (k n) -> p k n", k=8)
                for (lcb, L, colbase, blk) in blocks:
                    bg_step()
                    ps = PSM.get()
                    for kc in range(8):
                        mm(ps[0:L, 0:384], U[:, kc, lcb:lcb + L], sv[:, kc, :], kc == 0, kc == 7, [U, sl], ps)
                    while later:
                        later.pop(0)()
                    tm = TMP.get()
                    if s_ < 4:
                        act(tm[0:L, 0:384], ps[0:L, 0:384], AF.Copy, [ps], [tm], scale=(0.125 if s_ < 2 else 1.0))
                        v3 = tm[0:L, 0:384].rearrange("p (h e) -> p h e", e=64)
                        x1 = v3[:, :, 0:8]
                        x2 = v3[:, :, 8:16]
                        cs = ropet[0:L, blk, 0:8].unsqueeze(1).to_broadcast([L, 6, 8])
                        sn = ropet[0:L, blk, 8:16].unsqueeze(1).to_broadcast([L, 6, 8])
                        tms = [SM.get() for _ in range(4)]
                        tv = [t_[0:L, 0:48].rearrange("p (h e) -> p h e", e=8) for t_ in tms]
                        tt("dve", tv[0], x1, cs, ALU.mult, [tm, ropet], [tms[0]])
                        tt("dve", tv[1], x2, sn, ALU.mult, [tm, ropet], [tms[1]])
                        tt("dve", tv[2], x2, cs, ALU.mult, [tm, ropet], [tms[2]])
                        tt("dve", tv[3], x1, sn, ALU.mult, [tm, ropet], [tms[3]])
                        tt("dve", x1, tv[0], tv[1], ALU.subtract, [tms[0], tms[1]], [tm])
                        tt("dve", x2, tv[2], tv[3], ALU.add, [tms[2], tms[3]], [tm])
                        if s_ >= 2:
                            kv_out(tm, L, colbase, s_, L == NS)
                        tb = BT.get()
                        cp("act", tb[0:L, 0:384], tm[0:L, 0:384], [tm], [tb])
                        def fin(tb=tb, L=L, s_=s_, lcb=lcb, colbase=colbase):
                            pt_ = PST.get()
                            pb = pt_[:, :].bitcast(BF16)
                            for q in range(3):
                                trp(pb[:, q * L:(q + 1) * L], tb[0:L, q * 128:(q + 1) * 128], identb[0:L, 0:L], [tb, identb], pt_)
                            f0 = (s_ % 2) * 3
                            if s_ < 2:
                                cp("dve", QT[:, f0:f0 + 3, lcb:lcb + L], pb[:, 0:3 * L].rearrange("p (f t) -> p f t", f=3), [pt_], [QT])
                            else:
                                cp("dve", KTA[:, f0:f0 + 3, colbase:colbase + L], pb[:, 0:3 * L].rearrange("p (f t) -> p f t", f=3), [pt_], BALL)
                        later.append(fin)
                    else:
                        act(tm[0:L, 0:384], ps[0:L, 0:384], AF.Copy, [ps], [tm])
                        S.dma("sp", vscr[colbase:colbase + L, (s_ - 4) * 384:(s_ - 3) * 384], tm[0:L, 0:384], reads=[tm], writes=[VSCR], key=tm)
                        kv_out(tm, L, colbase, s_, L == NS)
            if ADBG < 2:
                return
            while later:
                later.pop(0)()
            units = []

            def mk_g0(c):
                def f():
                    tb_ = ti * 4 + c
                    kbs = []
                    if tb_ > 0:
                        rows = slice((tb_ - 1) * 128, tb_ * 128)
                        kbs.append((kt_all(0, rows), 128, load_va(rows, 0, 128), trige_b[:, :], BALL))
                    rows = slice(tb_ * 128, (tb_ + 1) * 128)
                    kbs.append((kt_all(0, rows), 128, load_va(rows, 0, 128), trile_b[:, :], BALL))
                    return (0, slice(c * 128, (c + 1) * 128), 128, kbs, slice(c * 128, (c + 1) * 128))
                return f

            def mk_g1(r):
                def f():
                    kbs = []
                    if ti > 0:
                        rows = slice((ti - 1) * 512 + r, ti * 512, 4)
                        kbs.append((kt_all(1, rows), 128, load_va(rows, 1, 128), trige_b[:, :], BALL))
                    rows = slice(ti * 512 + r, (ti + 1) * 512, 4)
                    kbs.append((kt_all(1, rows), 128, load_va(rows, 1, 128), trile_b[:, :], BALL))
                    return (1, slice(r, 512, 4), 128, kbs, slice(r, 512, 4))
                return f

            def mk_g2(r):
                def f():
                    nk = 32 * (ti + 1)
                    rows = slice(r, (ti + 1) * 512, 16)
                    kbs = [(kt_all(2, rows), nk, load_va(rows, 2, nk), trile_b[0:nk, 32 * ti:32 * ti + 32], BALL)]
                    return (2, slice(r, 512, 16), 32, kbs, slice(r, 512, 16))
                return f

            def mk_s(n, g):
                def f():
                    nm, W, dil = caches[g]
                    cach = I["c" + nm][j][n]
                    kc_ = VA.get()
                    S.dma("pool", kc_[:, :, :], cach[0:W:dil, 0:256].rearrange("p (h e) -> p h e", e=64), writes=[kc_])
                    vac = VA.get()
                    S.dma("pool", vac[:, :, :], cach[0:W:dil, 256:512].rearrange("p (h e) -> p h e", e=64), writes=[vac])
                    ktc = VA.get()
                    pt_ = PST.get()
                    pb = pt_[:, :].bitcast(BF16)
                    kcf = kc_[:, :, :].rearrange("p h e -> p (h e)")
                    for q in range(2):
                        trp(pb[:, q * 128:(q + 1) * 128], kcf[:, q * 128:(q + 1) * 128], identb[:], [kc_, identb], pt_)
                    ktf = ktc[:, :, :].rearrange("p h e -> p (h e)")
                    cp("dve", ktf[:, :], pb[:, 0:256], [pt_], [ktc])

                    def ktc_fn(hh, ktf=ktf):
                        pb_ = 64 * (hh % 2)
                        return ktf[pb_:pb_ + 64, (hh // 2) * 128:(hh // 2) * 128 + 128]
                    van = load_va(slice(TP, T), g, NS)
                    kbs = [(ktc_fn, 128, vac, None, [ktc]),
                           (kt_all(g, slice(TP, T)), NS, van, identb[0:NS, n:n + 1], BALL)]
                    return (g, slice(512 + n, 513 + n), 1, kbs, slice(512 + n, 513 + n))
                return f

            units += [mk_g0(c) for c in range(4)]
            units += [mk_g1(r) for r in range(4)]
            units += [mk_g2(r) for r in range(16)]
            if ti == 3:
                units += [mk_s(n, g) for g in range(3) for n in range(NS)]
            run_units(units)
            if ADBG < 7:
                continue
            for st in TILES[ti]:
                c0, w, lc = st
                act(DENT[0:64, :, lc:lc + w], DENT[0:64, :, lc:lc + w], AF.Ln, [DENT], [DENT])
                act(DENT[0:64, :, lc:lc + w], DENT[0:64, :, lc:lc + w], AF.Exp, [DENT], [DENT], scale=-1.0)
                for g in range(3):
                    tt("dve" if g != 1 else "pool", OB[0:64, g * 4:g * 4 + 4, lc:lc + w], OB[0:64, g * 4:g * 4 + 4, lc:lc + w],
                       DENT[0:64, :, lc:lc + w], ALU.mult, [OB, DENT], [OB])
            Unext = prenorm(ti + 1, i * 6 + 0) if (ti < 3 and HOIST) else None
            if ti == 3:
                hoist_next_stage()
            SQ = post_begin(ti)
            for s_ in range(4):
                sl = take(("wo", i, ti, s_))
                sv = sl[0:64, 0:12 * 256].rearrange("p (k n) -> p k n", k=12)
                for m2 in range(2):
                    mc = s_ * 2 + m2
                    for st in TILES[ti]:
                        c0, w, lc = st
                        pf = PSM.get()
                        for H in range(12):
                            mm(pf[:, :w], sv[:, H, m2 * 128:(m2 + 1) * 128], OB[0:64, H, lc:lc + w], H == 0, H == 11, [sl, OB], pf)
                        post_evac(pf, mc, st, SQ)
            post_finish(ti, i * 6 + 1, SQ)

    def mixer(i):
        if i % 2 == 0:
            ssd(i)
        else:
            att(i)

    for j in range(2):
        for nm, W in (("128", 128), ("512", 512), ("2048", 2048)):
            S.dma("sp", O["o_s" + nm][j][:, 0:W - 1, :], I["c" + nm][j][:, 1:W, :], key="outc")
        S.dma("sp", O["o_sconv"][j][:, 0:2, :], I["st_conv"][j][:, 1:3, :], key="outc")
    for k_, (kind, i) in enumerate(stages):
        xst["k"] = k_
        if kind == "mix":
            mixer(i)
        elif kind == "xa":
            xattn(i)
        else:
            ffn(i)
    assert DBG < 99 or ADBG < 99 or wst["taken"] == len(plan), (wst["taken"], len(plan))

    bg_drain()
    for tb in range(16):
        for hh in range(2):
            st_ = STG.get()
            ps = PSM.get()
            for q in range(4):
                kc = hh * 4 + q
                trp(ps[:, q * 128:(q + 1) * 128], xT[:, kc, tb * 128:(tb + 1) * 128], identf[:], [XT[tb // 4], identf], ps)
            cp("act" if hh else "dve", st_[:, 0:512], ps[:, :], [ps], [st_])
            S.dma("sp", O["y_p"][tb * 128:(tb + 1) * 128, hh * 512:(hh + 1) * 512], st_[:, 0:512], reads=[st_], key=st_)
    for hh in range(2):
        st_ = STG.get()
        ps = PSM.get()
        for q in range(4):
            kc = hh * 4 + q
            trp(ps[0:NS, q * 128:(q + 1) * 128], xT[:, kc, TP:T], identf[:], [XT[3], identf], ps)
        cp("dve", st_[0:NS, 0:512], ps[0:NS, :], [ps], [st_])
        S.dma("sp", O["y_s"][:, hh * 512:(hh + 1) * 512], st_[0:NS, 0:512], reads=[st_], key=st_)

    S.finish()
    S.emit()
    es.close()
    return nc


def rope_table():
    half = 8
    inv = (np.float32(500000.0) ** (-(np.arange(half, dtype=np.float32) / np.float32(half)))).astype(np.float32)
    pos = np.concatenate([np.arange(TP, dtype=np.float32), np.full((NS,), 8192.0, np.float32)])
    ang = (pos[:, None] * inv[None, :]).astype(np.float32)
    return np.concatenate([np.cos(ang), np.sin(ang)], axis=1).astype(np.float32)


def make_in_maps(inp, cores):
    f = lambda a: np.ascontiguousarray(np.asarray(a, dtype=np.float32))
    shared = {
        "norms": f(inp["norms"]).reshape(24, D), "ssm_w_in": f(inp["ssm_w_in"]),
        "ssm_conv_w": f(inp["ssm_conv_w"]).reshape(8, CONV), "ssm_conv_b": f(inp["ssm_conv_b"]),
        "ssm_dt_bias": f(inp["ssm_dt_bias"]), "ssm_a_log": f(inp["ssm_a_log"]), "ssm_d": f(inp["ssm_d"]),
        "ssm_norm_w": f(inp["ssm_norm_w"]), "ssm_w_out": f(inp["ssm_w_out"]), "att_w_qkv": f(inp["att_w_qkv"]),
        "att_w_o": f(inp["att_w_o"]), "mem_norm": f(inp["mem_norm"]), "xa_w_q": f(inp["xa_w_q"]),
        "xa_w_kv": f(inp["xa_w_kv"]), "xa_w_o": f(inp["xa_w_o"]), "ffn_w_gu": f(inp["ffn_w_gu"]),
        "ffn_conv_w": f(inp["ffn_conv_w"]).reshape(12, DFF), "ffn_conv_b": f(inp["ffn_conv_b"]),
        "ffn_w_down": f(inp["ffn_w_down"]), "rope": rope_table(),
    }
    maps = []
    for c in cores:
        s = slice(NS * c, NS * c + NS)
        m = dict(shared)
        m["xp"] = f(inp["x_prompt"][c])
        m["xs"] = f(inp["x_sample"][s, 0])
        m["mem"] = f(inp["mem_prompt"][c])
        m["st_ssm"] = f(inp["state_ssm"][:, s]).reshape(2, NS, 2048, 128)
        m["st_conv"] = f(inp["state_ssm_conv"][:, s])
        m["c128"] = f(inp["cache_swa_kv_w128"][:, s]).reshape(2, NS, 128, 512)
        m["c512"] = f(inp["cache_swa_kv_w512"][:, s]).reshape(2, NS, 512, 512)
        m["c2048"] = f(inp["cache_swa_kv_w2048"][:, s]).reshape(2, NS, 2048, 512)
        m["cmem"] = f(inp["cache_mem_kv"][:, s]).reshape(4, NS, 256, 2048)
        m["st_ffn"] = f(inp["state_ffn_conv"][:, s])
        maps.append(m)
    return maps


def gather(results):
    R = results
    n = len(R)
    cat = lambda k, ax: np.concatenate([np.asarray(r[k]) for r in R], axis=ax)
    stk = lambda k: np.stack([np.asarray(r[k]) for r in R], axis=1)
    y_p = np.stack([np.asarray(r["y_p"]) for r in R], 0)
    y_s = cat("y_s", 0).reshape(n * NS, 1, D)
    return (
        y_p, y_s,
        stk("o_pssm").reshape(2, n, 32, 64, 128), stk("o_pconv"),
        stk("o_p128").reshape(2, n, 128, 2, 4, 64), stk("o_p512").reshape(2, n, 512, 2, 4, 64),
        stk("o_p2048").reshape(2, n, 2048, 2, 4, 64), stk("o_pmem").reshape(4, n, 256, 2, 4, 256), stk("o_pffn"),
        cat("o_sssm", 1).reshape(2, n * NS, 32, 64, 128), cat("o_sconv", 1),
        cat("o_s128", 1).reshape(2, n * NS, 128, 2, 4, 64), cat("o_s512", 1).reshape(2, n * NS, 512, 2, 4, 64),
        cat("o_s2048", 1).reshape(2, n * NS, 2048, 2, 4, 64), cat("o_sffn", 1),
    )


def kernel(**inputs):
    nc = build()
    maps = make_in_maps(inputs, list(range(NCORES)))
    res = run_bass_kernel_spmd(nc, maps, core_ids=list(range(NCORES)))
    return tuple(np.ascontiguousarray(a, dtype=np.float32) for a in gather(res.results))
```
